# Optimizing a Trainium2 kernel written in Bass

```python
import jax, jax.numpy as jnp
from jax import lax
import numpy as np

D_MODEL = 4096
BATCH = 4
SEQ = 4096
DEPTH = 1

HEAD_DIM = 128
SB_HEADS = D_MODEL // 256
DIL_HEADS = D_MODEL // 512
DIL_GROUPS = ((128, 1), (512, 4), (2048, 16))
N_DIL = len(DIL_GROUPS)
D_SB = SB_HEADS * HEAD_DIM
D_DIL = DIL_HEADS * HEAD_DIM
D_FF = 4 * D_MODEL
Q_BLOCK = 128
ROPE_THETA = 10000.0
EPS = 1e-6
D_IN = 3 * D_SB + 3 * N_DIL * D_DIL + 2 * D_MODEL

kernel_name = 'hybrid_stickbreak_dilated_gated_block'


def rms_norm(x, w):
    xf = x.astype(jnp.float32)
    y = xf * lax.rsqrt(jnp.mean(xf * xf, axis=-1, keepdims=True) + EPS)
    return (y * w.astype(jnp.float32)).astype(x.dtype)


def rope(x, pos):
    half = HEAD_DIM // 2
    inv_freq = ROPE_THETA ** (-jnp.arange(half, dtype=jnp.float32) / half)
    ang = pos.astype(jnp.float32)[..., None] * inv_freq
    ang = jnp.expand_dims(ang, tuple(range(2, x.ndim - 1)))
    cos, sin = jnp.cos(ang), jnp.sin(ang)
    xf = x.astype(jnp.float32)
    x1, x2 = xf[..., :half], xf[..., half:]
    return jnp.concatenate([x1 * cos - x2 * sin, x2 * cos + x1 * sin], axis=-1).astype(x.dtype)


def to_blocks(t):
    b, s = t.shape[:2]
    return jnp.moveaxis(t.reshape(b, s // Q_BLOCK, Q_BLOCK, *t.shape[2:]), 1, 0)


def from_blocks(t):
    t = jnp.moveaxis(t, 0, 1)
    return t.reshape(t.shape[0], t.shape[1] * t.shape[2], *t.shape[3:])


def stick_breaking_attention(q, k, v):
    s_len, d = q.shape[1], q.shape[-1]
    scale = d ** -0.5
    key_pos = jnp.arange(s_len)

    def block(args):
        i, q_blk = args
        z = jnp.einsum('bqhd,bshd->bhqs', q_blk, k).astype(jnp.float32) * scale
        q_pos = i * Q_BLOCK + jnp.arange(Q_BLOCK)
        causal = key_pos[None, :] < q_pos[:, None]
        log_rest = jnp.where(causal, jax.nn.log_sigmoid(-z), 0.0)
        suffix = lax.cumsum(log_rest, axis=3, reverse=True) - log_rest
        a = jnp.where(causal, jnp.exp(jax.nn.log_sigmoid(z) + suffix), 0.0)
        return jnp.einsum('bhqs,bshd->bqhd', a.astype(v.dtype), v)

    out = lax.map(block, (jnp.arange(s_len // Q_BLOCK), to_blocks(q)))
    return from_blocks(out)


def dilated_attention(q, k, v):
    s_len = q.shape[1]
    scale = HEAD_DIM ** -0.5
    outs, lses = [], []
    for g, (window, dilation) in enumerate(DIL_GROUPS):
        n_taps = window // dilation + 1
        kg, vg = k[:, :, g], v[:, :, g]

        def block(args, kg=kg, vg=vg, dilation=dilation, n_taps=n_taps):
            i, q_blk = args
            q_pos = i * Q_BLOCK + jnp.arange(Q_BLOCK)
            idx = q_pos[:, None] - dilation * jnp.arange(n_taps)[None, :]
            valid = idx >= 0
            idx = jnp.maximum(idx, 0)
            k_sel = jnp.take(kg, idx, axis=1)
            v_sel = jnp.take(vg, idx, axis=1)
            sc = jnp.einsum('bqhd,bqkhd->bhqk', q_blk, k_sel).astype(jnp.float32) * scale
            sc = jnp.where(valid[None, None], sc, -jnp.inf)
            lse = jax.nn.logsumexp(sc, axis=-1)
            p = jnp.exp(sc - lse[..., None])
            o = jnp.einsum('bhqk,bqkhd->bqhd', p.astype(vg.dtype), v_sel)
            return o, jnp.moveaxis(lse, 1, 2)

        o_g, lse_g = lax.map(block, (jnp.arange(s_len // Q_BLOCK), to_blocks(q[:, :, g])))
        outs.append(from_blocks(o_g))
        lses.append(from_blocks(lse_g))
    o = jnp.stack(outs, axis=2)
    w = jax.nn.softmax(jnp.stack(lses, axis=2), axis=2)
    return jnp.sum(o * w[..., None].astype(o.dtype), axis=2)


def token_mixer(h, positions, w_in, w_o_sb, w_o_dil, w_out):
    b, s, _ = h.shape
    proj = h @ w_in
    splits = np.cumsum([D_SB, D_SB, D_SB, N_DIL * D_DIL, N_DIL * D_DIL, N_DIL * D_DIL, D_MODEL]).tolist()
    qa, ka, va, qb, kb, vb, ga, gb = jnp.split(proj, splits, axis=-1)
    sb_shape = (b, s, SB_HEADS, HEAD_DIM)
    dil_shape = (b, s, N_DIL, DIL_HEADS, HEAD_DIM)
    y_a = stick_breaking_attention(qa.reshape(sb_shape), ka.reshape(sb_shape), va.reshape(sb_shape))
    y_a = y_a.reshape(b, s, D_SB) @ w_o_sb
    qb = rope(qb.reshape(dil_shape), positions)
    kb = rope(kb.reshape(dil_shape), positions)
    y_b = dilated_attention(qb, kb, vb.reshape(dil_shape))
    y_b = y_b.reshape(b, s, D_DIL) @ w_o_dil
    merged = jax.nn.sigmoid(ga) * y_a + jax.nn.sigmoid(gb) * y_b
    return merged @ w_out


def setup_inputs(seed: int = 0) -> dict:
    key = jax.random.key(seed)
    ks = jax.random.split(key, 14)
    f32 = jnp.float32
    nrm = lambda k, shape, fan_in, gain=1.0: (jax.random.normal(k, shape, f32) * (gain * fan_in ** -0.5)).astype(f32)
    return {
        'x': jax.random.normal(ks[0], (BATCH, SEQ, D_MODEL), f32),
        'c': jax.random.normal(ks[1], (BATCH, D_MODEL), f32),
        'positions': jnp.broadcast_to(jnp.arange(SEQ, dtype=jnp.int32), (BATCH, SEQ)),
        'ada_w': nrm(ks[2], (DEPTH, D_MODEL, 6 * D_MODEL), D_MODEL, 0.5),
        'ada_b': 0.01 * jax.random.normal(ks[3], (DEPTH, 6 * D_MODEL), f32),
        'norm_mix_w': 1.0 + 0.02 * jax.random.normal(ks[4], (DEPTH, D_MODEL), f32),
        'w_in': nrm(ks[5], (DEPTH, D_MODEL, D_IN), D_MODEL),
        'w_o_sb': nrm(ks[6], (DEPTH, D_SB, D_MODEL), D_SB),
        'w_o_dil': nrm(ks[7], (DEPTH, D_DIL, D_MODEL), D_DIL),
        'w_out': nrm(ks[8], (DEPTH, D_MODEL, D_MODEL), D_MODEL),
        'norm_mlp_w': 1.0 + 0.02 * jax.random.normal(ks[9], (DEPTH, D_MODEL), f32),
        'w_ff1': nrm(ks[10], (DEPTH, D_MODEL, D_FF), D_MODEL),
        'w_ff2': nrm(ks[11], (DEPTH, D_FF, D_MODEL), D_FF),
        'norm_final_w': 1.0 + 0.02 * jax.random.normal(ks[12], (D_MODEL,), f32),
    }


def reference(x, c, positions, ada_w, ada_b, norm_mix_w, w_in, w_o_sb, w_o_dil, w_out,
              norm_mlp_w, w_ff1, w_ff2, norm_final_w):
    for l in range(DEPTH):
        mod = jax.nn.silu(c) @ ada_w[l] + ada_b[l]
        sh1, sc1, g1, sh2, sc2, g2 = [m[:, None, :] for m in jnp.split(mod, 6, axis=-1)]
        h = rms_norm(x, norm_mix_w[l]) * (1 + sc1) + sh1
        x = x + g1 * token_mixer(h, positions, w_in[l], w_o_sb[l], w_o_dil[l], w_out[l])
        h = rms_norm(x, norm_mlp_w[l]) * (1 + sc2) + sh2
        x = x + g2 * (jnp.square(jax.nn.relu(h @ w_ff1[l])) @ w_ff2[l])
    return rms_norm(x, norm_final_w)
```

```python
import math
import contextlib
import numpy as np
import ml_dtypes
import concourse.bass as bass
import concourse.mybir as mybir
from concourse.bass_utils import run_bass_kernel_spmd

F32 = mybir.dt.float32
BF16 = mybir.dt.bfloat16
I32 = mybir.dt.int32
AF = mybir.ActivationFunctionType
ALU = mybir.AluOpType
MAGIC = 12582912.0
GATES_IN_P3 = True
EPS = 1e-6


class Cfg:
    def __init__(self, D, T, NH_SB, NH_DIL, groups, DFF):
        self.D, self.T, self.L = D, T, 2 * T
        self.NH_SB, self.NH_DIL, self.groups, self.DFF = NH_SB, NH_DIL, groups, DFF
        self.KC = D // 128
        self.D_SB = NH_SB * 128
        self.D_DIL = NH_DIL * 128
        self.NG = len(groups)
        self.D_IN = 3 * self.D_SB + 3 * self.NG * self.D_DIL + 2 * D
        self.NT = self.L // 128
        self.NTO = T // 128
        self.TS = min(1024, T)
        self.PW = min(512, self.TS)
        self.TQ = min(512, T)
        self.TF = min(512, T)
        self.taps = [(g, d) for g, (w, r) in enumerate(groups) for d in range(w // 128 + 1)]


REAL = Cfg(4096, 2048, 16, 8, ((128, 1), (512, 4), (2048, 16)), 16384)


class Buf:
    def __init__(self, name):
        self.name = name
        self.w = None
        self.r = []


class Eng:
    def __init__(self, name, sem, dsems):
        self.name, self.sem, self.dsems = name, sem, dsems
        self.cnt = 0
        self.dcnt = [0] * len(dsems)
        self.dn = 0
        self.seen = {}
        self.ops = []


class Sched:
    def __init__(self, nc, engs):
        self.nc = nc
        self.E = engs
        self.all_tokens = {}

    def _wait(self, e, deps):
        best = {}
        for (sem, val, src) in deps:
            if src == 'pe' and e.name == 'pe':
                continue
            if id(sem) not in best or best[id(sem)][1] < val:
                best[id(sem)] = (sem, val)
        for (sem, val) in best.values():
            if e.seen.get(id(sem), 0) >= val:
                continue
            e.seen[id(sem)] = val
            e.ops.append(lambda eng, sem=sem, val=val: eng.wait_ge(sem, val))

    def begin(self, en, reads=(), writes=()):
        self._wait(self.E[en], self._deps(reads, writes))

    def raw(self, en, fn):
        self.E[en].ops.append(lambda eng, fn=fn: fn(eng))

    def _deps(self, reads, writes):
        deps = []
        for b in reads:
            if b.w is not None:
                deps.append(b.w)
        for b in writes:
            if b.w is not None:
                deps.append(b.w)
            deps.extend(b.r)
        return deps

    def _mark(self, tok, reads, writes):
        for b in reads:
            b.r.append(tok)
        for b in writes:
            b.w = tok
            b.r = []

    def group(self, en, fns, reads=(), writes=()):
        e = self.E[en]
        self._wait(e, self._deps(reads, writes))
        e.cnt += 1
        n = len(fns)
        for i, fn in enumerate(fns):
            if i == n - 1:
                e.ops.append(lambda eng, fn=fn, sem=e.sem: fn(eng).then_inc(sem, 1))
            else:
                e.ops.append(lambda eng, fn=fn: fn(eng))
        tok = (e.sem, e.cnt, en)
        self._mark(tok, reads, writes)
        return tok

    def op(self, en, fn, reads=(), writes=()):
        return self.group(en, [fn], reads, writes)

    def dma(self, en, out, in_, reads=(), writes=(), **kw):
        e = self.E[en]
        j = e.dn % len(e.dsems)
        e.dn += 1
        sem = e.dsems[j]
        deps = self._deps(reads, writes)
        if e.dcnt[j] > 0:
            deps.append((sem, e.dcnt[j], 'dma'))
        self._wait(e, deps)
        e.dcnt[j] += 16
        e.ops.append(lambda eng, out=out, in_=in_, sem=sem, kw=kw:
                     eng.dma_start(out=out, in_=in_, **kw).then_inc(sem, 16))
        tok = (sem, e.dcnt[j], 'dma')
        self._mark(tok, reads, writes)
        return tok

    def barrier(self, bufs=()):
        toks = []
        for e in self.E.values():
            if e.cnt > 0:
                toks.append((e.sem, e.cnt, 'bar'))
            for j, s in enumerate(e.dsems):
                if e.dcnt[j] > 0:
                    toks.append((s, e.dcnt[j], 'bar'))
        for e in self.E.values():
            self._wait(e, toks)
        for b in bufs:
            b.w = None
            b.r = []


def build(cfg, debug_outs=()):
    c = cfg
    D, T, L, KC = c.D, c.T, c.L, c.KC
    nc = bass.Bass("TRN2", target_bir_lowering=False)

    def din(name, shape, dt):
        return nc.dram_tensor(name, list(shape), dt, kind="ExternalInput")

    def dscr(name, shape, dt):
        kind = "ExternalOutput" if name in debug_outs else "Internal"
        return nc.dram_tensor(name, list(shape), dt, kind=kind)

    NTAP = len(c.taps)
    xl = din("xl", [L, D], F32)
    c_t = din("c_t", [128, KC], F32)
    pos_t = din("pos_t", [128, c.NT], I32)
    ada_w = din("ada_w", [D, 6 * D], F32)
    adab_t = din("adab_t", [128, 6 * KC], F32)
    nw1_t = din("nw1_t", [128, KC], F32)
    nw2_t = din("nw2_t", [128, KC], F32)
    nfw_t = din("nfw_t", [128, KC], F32)
    w_in = din("w_in", [D, c.D_IN], F32)
    w_o_sb = din("w_o_sb", [c.D_SB, D], F32)
    w_o_dil = din("w_o_dil", [c.D_DIL, D], F32)
    w_out = din("w_out", [D, D], F32)
    w_ff1 = din("w_ff1", [D, c.DFF], F32)
    w_ff2 = din("w_ff2", [c.DFF, D], F32)
    k_identf = din("k_identf", [128, 128], F32)
    k_identb = din("k_identb", [128, 128], BF16)
    k_negtri = din("k_negtri", [128, 128], BF16)
    k_negones = din("k_negones", [128, 128], BF16)
    k_onesf = din("k_onesf", [128, 128], F32)
    k_sbmask = din("k_sbmask", [128, c.TQ // 128, c.TQ], BF16)
    k_dmask = din("k_dmask", [128, NTAP, 128], BF16)
    k_dmaskc = din("k_dmaskc", [128, NTAP, 128], BF16)
    k_cbias = din("k_cbias", [128, 1], F32)
    k_invf = din("k_invf", [128, 64], F32)
    out_d = nc.dram_tensor("out", [T, D], F32, kind="ExternalOutput")
    hT_d = dscr("hT_d", [D, L], BF16)
    xT_d = dscr("xT_d", [D, T], F32)
    qT_d = dscr("qT_d", [c.D_SB, T], BF16)
    kT_d = dscr("kT_d", [c.D_SB, L], BF16)
    v_d = dscr("v_d", [L, c.D_SB], BF16)
    qdT_d = dscr("qdT_d", [c.NG * c.D_DIL, T], BF16)
    kdT_d = dscr("kdT_d", [c.NG * c.D_DIL, L], BF16)
    vd_d = dscr("vd_d", [L, c.NG * c.D_DIL], BF16)
    gT_d = dscr("gT_d", [2 * D, T], F32)
    yaT_d = dscr("yaT_d", [c.D_SB, T], BF16)
    ybT_d = dscr("ybT_d", [c.D_DIL, T], BF16)
    x1T_d = dscr("x1T_d", [D, T], F32)
    h2T_d = dscr("h2T_d", [D, T], BF16)
    mod_d = dscr("mod_d", [128, 6 * KC], F32)
    SWF = 256 if c.DFF >= 256 else 128
    NGF = c.DFF // SWF
    w1b_d = dscr("w1b_d", [NGF, 128, KC, SWF], BF16)
    w2b_d = dscr("w2b_d", [c.DFF, D], BF16)

    es = contextlib.ExitStack()
    with es:
        def sem(name):
            return es.enter_context(nc.semaphore(name))

        engs = {
            'pe': Eng('pe', sem('s_pe'), []),
            'act': Eng('act', sem('s_act'), []),
            'dve': Eng('dve', sem('s_dve'), []),
            'pool': Eng('pool', sem('s_pool'), [sem(f'd_pool{i}') for i in range(6)]),
            'sp': Eng('sp', sem('s_sp'), [sem(f'd_sp{i}') for i in range(12)]),
        }
        S = Sched(nc, engs)

        def sb(name, shape, dt, stack=es):
            return stack.enter_context(nc.sbuf_tensor(name, list(shape), dt))

        PS = [es.enter_context(nc.psum_tensor(f"ps{i}", [128, 512], F32)) for i in range(8)]
        PSB = [Buf(f"ps{i}") for i in range(8)]

        identf = sb("identf", [128, 128], F32)
        identb = sb("identb", [128, 128], BF16)
        negtri = sb("negtri", [128, 128], BF16)
        negones = sb("negones", [128, 128], BF16)
        onesf = sb("onesf", [128, 128], F32)
        cbias = sb("cbias", [128, 1], F32)
        invf = sb("invf", [128, 64], F32)
        c_eps = sb("c_eps", [128, 1], F32)
        c_one = sb("c_one", [128, 1], F32)
        c_zero = sb("c_zero", [128, 1], F32)
        c_pi = sb("c_pi", [128, 1], F32)
        modsb = sb("modsb", [128, 6 * KC], F32)
        A1 = sb("A1", [128, KC], F32)
        A2 = sb("A2", [128, KC], F32)
        nfwt = sb("nfwt", [128, KC], F32)
        sc_bf = sb("sc_bf", [128, KC], BF16)
        adab = sb("adab", [128, 6 * KC], F32)
        nw2 = sb("nw2", [128, KC], F32)
        tmpk = sb("tmpk", [128, KC], F32)
        MB = {n: Buf(n) for n in ["sc_bf", "adab", "nw", "tmpk", "mod1", "mod2"]}
        SW0 = 256
        nsl0 = 6 * D // SW0
        nfirst = 2 * D // SW0
        nd0 = [0]
        CONST = Buf("const")

        def mod_dma(upto, wsl, wslB):
            while nd0[0] <= min(upto, nsl0 - 1):
                i = nd0[0]
                dst, src = wload(wsl[i % 3][:, :, :], ada_w.ap(), 0, D, i * SW0, SW0)
                S.dma('pool', dst, src, writes=[wslB[i % 3]])
                nd0[0] += 1

        def mod_slab(sl, wsl, wslB, bank, last):
            wb, wB = wsl[sl % 3], wslB[sl % 3]
            mod_dma(min(sl + 2, last), wsl, wslB)
            for jj in range(SW0 // 128):
                col = sl * (SW0 // 128) + jj
                fns = [lambda e, kc=kc, jj=jj, col=col, wb=wb: e.matmul(
                    PS[bank][:, col:col + 1], lhsT=wb[:, kc, jj * 128:(jj + 1) * 128],
                    rhs=sc_bf[:, kc:kc + 1], start=(kc == 0), stop=(kc == KC - 1)) for kc in range(KC)]
                S.group('pe', fns, reads=[wB, MB["sc_bf"]], writes=[PSB[bank]])

        def mod_finish(c0, c1, Mb, Ax, nwx, SCx, bank):
            S.op('dve', lambda e: e.tensor_tensor(out=modsb[:, c0:c1], in0=PS[bank][:, c0:c1], in1=adab[:, c0:c1], op=ALU.add),
                 reads=[PSB[bank], MB["adab"]], writes=[Mb])
            S.op('dve', lambda e: e.tensor_scalar(out=tmpk[:, :], in0=modsb[:, SCx:SCx + KC], scalar1=1.0,
                                                  scalar2=None, op0=ALU.add), reads=[Mb], writes=[MB["tmpk"]])
            S.op('dve', lambda e: e.tensor_tensor(out=Ax[:, :], in0=tmpk[:, :], in1=nwx[:, :], op=ALU.mult),
                 reads=[MB["tmpk"], MB["nw"]], writes=[Mb])

        def ld_const(dst, src):
            S.dma('sp', dst, src, writes=[CONST])

        ld_const(identf[:, :], k_identf.ap())
        ld_const(identb[:, :], k_identb.ap())
        ld_const(negtri[:, :], k_negtri.ap())
        ld_const(negones[:, :], k_negones.ap())
        ld_const(onesf[:, :], k_onesf.ap())
        ld_const(cbias[:, :], k_cbias.ap())
        ld_const(invf[:, :], k_invf.ap())
        ld_const(nfwt[:, :], nfw_t.ap())
        S.op('dve', lambda e: e.memset(c_eps[:, :], EPS), writes=[CONST])
        S.op('dve', lambda e: e.memset(c_one[:, :], 1.0), writes=[CONST])
        S.op('dve', lambda e: e.memset(c_zero[:, :], 0.0), writes=[CONST])
        S.op('dve', lambda e: e.memset(c_pi[:, :], math.pi), writes=[CONST])

        SH1, SC1, G1, SH2, SC2, G2 = [i * KC for i in range(6)]

        def wload(dst, w_ap, r0, rows, c0, cols):
            return (dst, w_ap[r0:r0 + rows, c0:c0 + cols].rearrange("(k p) f -> p k f", p=128))

        conv_steps = []
        for fg_ in range(NGF):
            def cv1(fg_=fg_):
                src = w_ff1.ap()[:, fg_ * SWF:(fg_ + 1) * SWF].rearrange("(k p) f -> p k f", p=128)
                S.dma('pool', w1b_d.ap()[fg_], src)
            conv_steps.append(cv1)

            def cv2(fg_=fg_):
                npc = max(1, D // 2048)
                src = w_ff2.ap()[fg_ * SWF:(fg_ + 1) * SWF, :].rearrange("r (a f) -> r a f", a=npc)
                dst = w2b_d.ap()[fg_ * SWF:(fg_ + 1) * SWF, :].rearrange("r (a f) -> r a f", a=npc)
                S.dma('pool', dst, src)
            conv_steps.append(cv2)
        cpos = [0]

        def conv_run(n):
            for f_ in conv_steps[cpos[0]:cpos[0] + n]:
                f_()
            cpos[0] = min(len(conv_steps), cpos[0] + n)

        def norm_stats(xTt, xB, W, bank, sqs, sqB, rs, rstd, rB):
            for kc in range(KC):
                sq, sB = sqs[kc % 2], sqB[kc % 2]
                S.op('act', lambda e, kc=kc, sq=sq: e.activation(out=sq[:, 0:W], in_=xTt[:, kc, 0:W], func=AF.Square),
                     reads=[xB[kc * len(xB) // KC]], writes=[sB])
                S.op('pe', lambda e, kc=kc, sq=sq: e.matmul(PS[bank][:, 0:W], lhsT=onesf[:, :], rhs=sq[:, 0:W],
                                                            start=(kc == 0), stop=(kc == KC - 1)),
                     reads=[sB, CONST], writes=[PSB[bank]])
            S.op('act', lambda e: e.activation(out=rs[:, 0:W], in_=PS[bank][:, 0:W], func=AF.Sqrt, bias=c_eps[:, 0:1],
                                               scale=1.0 / D), reads=[PSB[bank], CONST], writes=[rB])
            S.op('dve', lambda e: e.reciprocal(out=rstd[:, 0:W], in_=rs[:, 0:W]), reads=[rB], writes=[rB])

        def norm_mod(ph, tag, xTt, xB, W, A, Bcol0, hst, hB, bank, sqs, sqB, tmps, tmpB, rs, rstd, rB, xr=()):
            norm_stats(xTt, xB, W, bank, sqs, sqB, rs, rstd, rB)
            for kc in range(KC):
                tm, tB = tmps[kc % 2], tmpB[kc % 2]
                S.op('dve', lambda e, kc=kc, tm=tm: e.scalar_tensor_tensor(
                    out=tm[:, 0:W], in0=xTt[:, kc, 0:W], scalar=A[:, kc:kc + 1], in1=rstd[:, 0:W],
                    op0=ALU.mult, op1=ALU.mult), reads=[xB[kc * len(xB) // KC], rB] + list(xr), writes=[tB])
                S.op('act', lambda e, kc=kc, tm=tm: e.activation(
                    out=hst[:, kc, 0:W], in_=tm[:, 0:W], func=AF.Identity,
                    bias=modsb[:, Bcol0 + kc:Bcol0 + kc + 1], scale=1.0), reads=[tB] + list(xr), writes=[hB])

        W1 = min(512, T)
        nb1 = W1 // 128
        with contextlib.ExitStack() as ph:
            c_sb = sb("c_sb", [128, KC], F32, ph)
            nw1 = sb("nw1", [128, KC], F32, ph)
            wsl = [sb(f"w0_{i}", [128, KC, SW0], BF16, ph) for i in range(3)]
            wslB = [Buf(f"w0_{i}") for i in range(3)]
            B = MB
            B["c_sb"] = Buf("c_sb")
            xts = [sb(f"xt{i}", [128, D], F32, ph) for i in range(2)]
            xtB = [Buf(f"xt{i}") for i in range(2)]
            xTt = sb("xTt", [128, KC, W1], F32, ph)
            xTBs = [Buf(f"xTt{i}") for i in range(max(1, KC // min(4, KC)))]
            hst = sb("hst", [128, KC, W1], BF16, ph)
            hB = Buf("hst")
            sqs = [sb(f"sq{i}", [128, W1], F32, ph) for i in range(2)]
            sqB = [Buf(f"sq{i}") for i in range(2)]
            tmps = [sb(f"tm{i}", [128, W1], F32, ph) for i in range(2)]
            tmpB = [Buf(f"tm{i}") for i in range(2)]
            rs = sb("rs", [128, W1], F32, ph)
            rstd = sb("rstd", [128, W1], F32, ph)
            rB = Buf("rs")

            S.dma('sp', c_sb[:, :], c_t.ap(), writes=[B["c_sb"]])
            S.dma('sp', adab[:, :], adab_t.ap(), writes=[B["adab"]])
            S.dma('sp', nw1[:, :], nw1_t.ap(), writes=[B["nw"]])
            S.dma('sp', nw2[:, :], nw2_t.ap(), writes=[B["nw"]])
            S.op('act', lambda e: e.activation(out=sc_bf[:, :], in_=c_sb[:, :], func=AF.Silu),
                 reads=[B["c_sb"]], writes=[B["sc_bf"]])
            for sl in range(nfirst):
                mod_slab(sl, wsl, wslB, 0, nfirst - 1)
            mod_finish(0, 2 * KC, B["mod1"], A1, nw1, SC1, 0)
            ntl = L // W1
            cpb = min(4, KC)
            ev = 0
            for tt in range(ntl):
                for j in range(nb1):
                    tb = tt * nb1 + j
                    xt, xB_ = xts[tb % 2], xtB[tb % 2]
                    S.dma('sp', xt[:, :], xl.ap()[tb * 128:(tb + 1) * 128, :], writes=[xB_])
                    for g4 in range(KC // cpb):
                        bk = 1 + (ev % 4)
                        fns = [lambda e, i=i, g4=g4, xt=xt, bk=bk: e.transpose(
                            PS[bk][:, i * 128:(i + 1) * 128], xt[:, (g4 * cpb + i) * 128:(g4 * cpb + i + 1) * 128],
                            identf[:, :]) for i in range(cpb)]
                        S.group('pe', fns, reads=[xB_, CONST], writes=[PSB[bk]])
                        src = PS[bk][:, 0:cpb * 128].rearrange("p (a b) -> p a b", a=cpb)
                        dst = xTt[:, g4 * cpb:(g4 + 1) * cpb, j * 128:(j + 1) * 128]
                        if ev % 2 == 0:
                            S.op('act', lambda e, dst=dst, src=src: e.activation(out=dst, in_=src, func=AF.Copy),
                                 reads=[PSB[bk]], writes=[xTBs[g4]])
                        else:
                            S.op('dve', lambda e, dst=dst, src=src: e.tensor_copy(out=dst, in_=src),
                                 reads=[PSB[bk]], writes=[xTBs[g4]])
                        ev += 1
                norm_mod(ph, "n1", xTt, xTBs, W1, A1, SH1, hst, hB, 5, sqs, sqB, tmps, tmpB, rs, rstd, rB, xr=[B["mod1"]])
                S.dma('sp', hT_d.ap()[:, tt * W1:(tt + 1) * W1].rearrange("(k p) t -> p k t", p=128), hst[:, :, :],
                      reads=[hB])
                if tt * W1 >= T:
                    o0 = tt * W1 - T
                    S.dma('sp', xT_d.ap()[:, o0:o0 + W1].rearrange("(k p) t -> p k t", p=128), xTt[:, :, :],
                          reads=xTBs)
            S.barrier(PSB + [CONST])

        TS, PW = c.TS, c.PW
        NHF = TS // PW
        NTB = TS // 128
        o_qa = 0
        o_ka = c.D_SB
        o_va = 2 * c.D_SB
        o_qb = 3 * c.D_SB
        DG = c.NG * c.D_DIL
        o_kb = o_qb + DG
        o_vb = o_kb + DG
        o_ga = o_vb + DG
        o_gb = o_ga + D
        segs = [("qa", o_qa, c.D_SB, "fm", False), ("ka", o_ka, c.D_SB, "fm", True), ("va", o_va, c.D_SB, "tm", True),
                ("qb", o_qb, DG, "rope", False), ("kb", o_kb, DG, "rope", True), ("vb", o_vb, DG, "tm", True),
                ("ga", o_ga, D, "fm", False), ("gb", o_gb, D, "fm", False)]
        with contextlib.ExitStack() as ph:
            cosT = sb("cosT", [128, c.NT, 64], F32, ph)
            sinT = sb("sinT", [128, c.NT, 64], F32, ph)
            cosQ = sb("cosQ", [128, c.NTO, 64], F32, ph)
            sinQ = sb("sinQ", [128, c.NTO, 64], F32, ph)
            with contextlib.ExitStack() as ph0:
                phx = ph
                ph = ph0
                posi = sb("posi", [128, c.NT], I32, ph)
                posf = sb("posf", [128, c.NT], F32, ph)
                ang = sb("ang", [128, c.NT, 64], F32, ph)
                t1 = sb("rt1", [128, c.NT, 64], F32, ph)
                t2 = sb("rt2", [128, c.NT, 64], F32, ph)
                B = {n: Buf(n) for n in ["posi", "posf", "ang", "t1", "t2"]}
                S.dma('sp', posi[:, :], pos_t.ap(), writes=[B["posi"]])
                S.op('dve', lambda e: e.tensor_copy(out=posf[:, :], in_=posi[:, :]), reads=[B["posi"]], writes=[B["posf"]])
                for t in range(c.NT):
                    S.op('dve', lambda e, t=t: e.tensor_scalar(out=ang[:, t, :], in0=invf[:, :], scalar1=posf[:, t:t + 1],
                                                               scalar2=None, op0=ALU.mult),
                         reads=[B["posf"], CONST], writes=[B["ang"]])
                C1 = 6.28125
                C2 = 2.0 * math.pi - C1
                INV2PI = 1.0 / (2.0 * math.pi)

                def sin_table(dst, shift):
                    if shift != 0.0:
                        S.op('dve', lambda e: e.tensor_scalar(out=t2[:, :, :], in0=ang[:, :, :], scalar1=shift, scalar2=None,
                                                              op0=ALU.add), reads=[B["ang"]], writes=[B["t2"]])
                        th, thB = t2, B["t2"]
                    else:
                        th, thB = ang, B["ang"]
                    S.op('dve', lambda e: e.tensor_scalar(out=t1[:, :, :], in0=th[:, :, :], scalar1=INV2PI, scalar2=MAGIC,
                                                          op0=ALU.mult, op1=ALU.add), reads=[thB], writes=[B["t1"]])
                    S.op('dve', lambda e: e.tensor_scalar(out=t1[:, :, :], in0=t1[:, :, :], scalar1=-MAGIC, scalar2=None,
                                                          op0=ALU.add), reads=[B["t1"]], writes=[B["t1"]])
                    S.op('dve', lambda e: e.scalar_tensor_tensor(out=th[:, :, :] if th is t2 else t2[:, :, :], in0=t1[:, :, :],
                                                                 scalar=-C1, in1=th[:, :, :], op0=ALU.mult, op1=ALU.add),
                         reads=[B["t1"], thB], writes=[B["t2"]])
                    S.op('dve', lambda e: e.scalar_tensor_tensor(out=t2[:, :, :], in0=t1[:, :, :], scalar=-C2, in1=t2[:, :, :],
                                                                 op0=ALU.mult, op1=ALU.add),
                         reads=[B["t1"], B["t2"]], writes=[B["t2"]])
                    S.op('dve', lambda e: e.tensor_scalar(out=t2[:, :, :], in0=t2[:, :, :], scalar1=3.14159, scalar2=-3.14159,
                                                          op0=ALU.min, op1=ALU.max), reads=[B["t2"]], writes=[B["t2"]])
                    S.op('act', lambda e: e.activation(out=dst[:, :, :], in_=t2[:, :, :], func=AF.Sin),
                         reads=[B["t2"]], writes=[CONST])

                sin_table(sinT, 0.0)
                sin_table(cosT, math.pi / 2.0)
                qs = 128.0 ** -0.5
                S.op('dve', lambda e: e.tensor_scalar(out=cosQ[:, :, :], in0=cosT[:, c.NTO:, :], scalar1=qs, scalar2=None,
                                                      op0=ALU.mult), reads=[CONST], writes=[CONST])
                S.op('dve', lambda e: e.tensor_scalar(out=sinQ[:, :, :], in0=sinT[:, c.NTO:, :], scalar1=qs, scalar2=None,
                                                      op0=ALU.mult), reads=[CONST], writes=[CONST])
                S.barrier(PSB + [CONST])
                ph = phx
            hTt = sb("hTt", [128, KC, TS], BF16, ph)
            hTB = Buf("hTt")
            SW = 256
            wsl = [sb(f"w2_{i}", [128, KC, SW], BF16, ph) for i in range(3)]
            wslB = [Buf(f"w2_{i}") for i in range(3)]
            stb = [sb(f"stb{i}", [128, 4, TS], BF16, ph) for i in range(2)]
            stbB = [Buf(f"stb{i}") for i in range(2)]
            stf = sb("stf", [128, 4, TS], F32, ph)
            stfB = Buf("stf")
            stt = [sb(f"stt{i}", [128, NTB, SW], BF16, ph) for i in range(2)]
            sttB = [Buf(f"stt{i}") for i in range(2)]
            rp = [sb(f"rp{i}", [128, SW], BF16, ph) for i in range(2)]
            rpB = [Buf(f"rp{i}") for i in range(2)]
            ra = [sb(f"ra{i}", [128, SW // 2], F32, ph) for i in range(2)]
            raB = [Buf(f"ra{i}") for i in range(2)]
            nsl = 0
            nbk = 0
            nst = 0
            nrp = 0
            NW = 3
            slist = []
            for st in range(L // TS):
                own = st * TS >= T
                t0 = st * TS
                for (nm, o0, wd, kind, allt) in segs:
                    if not own and not allt:
                        continue
                    if GATES_IN_P3 and nm in ("ga", "gb"):
                        continue
                    for s0 in range(0, wd, min(SW, wd)):
                        sw = min(SW, wd - s0)
                        tb_lo = 0
                        if (not own) and nm in ("kb", "vb"):
                            g_lo, g_hi = s0 // c.D_DIL, (s0 + sw - 1) // c.D_DIL
                            wmax = max(c.groups[g][0] for g in range(g_lo, g_hi + 1))
                            tb_lo = max(0, (T - wmax - t0) // 128)
                            if tb_lo >= NTB:
                                continue
                        slist.append((st, nm, o0, wd, kind, s0, sw, tb_lo))
            ndma = [0]

            def ensure_dma(upto):
                while ndma[0] <= min(upto, len(slist) - 1):
                    i = ndma[0]
                    (st_, nm_, o0_, wd_, kind_, s0_, sw_, tbl_) = slist[i]
                    dst, src = wload(wsl[i % NW][:, :, 0:sw_], w_in.ap(), 0, D, o0_ + s0_, sw_)
                    S.dma('pool', dst, src, writes=[wslB[i % NW]])
                    ndma[0] += 1

            cur_st = -1
            for si, sl_ in enumerate(slist):
                for _o1 in (0,):
                    for _o2 in (0,):
                        (st, nm, o0, wd, kind, s0, sw, tb_lo) = sl_
                        own = st * TS >= T
                        t0 = st * TS
                        to0 = t0 - T
                        ensure_dma(si + 2)
                        if si % 4 == 3:
                            conv_run(1)
                        if st != cur_st:
                            cur_st = st
                            for k4 in range(0, KC, 8):
                                k4e = min(KC, k4 + 8)
                                S.dma('sp', hTt[:, k4:k4e, :],
                                      hT_d.ap()[k4 * 128:k4e * 128, t0:t0 + TS].rearrange("(k p) t -> p k t", p=128),
                                      writes=[hTB])
                        wb, wB = wsl[si % NW], wslB[si % NW]
                        nch = sw // 128
                        if kind == "fm":
                            gate = nm in ("ga", "gb")
                            if gate:
                                stg, sgB = stf, stfB
                            else:
                                stg, sgB = stb[nst % 2], stbB[nst % 2]
                                nst += 1
                            for ci in range(nch):
                                for hf in range(NHF):
                                    bk = nbk % 6
                                    nbk += 1
                                    fns = [lambda e, kc=kc, ci=ci, hf=hf, wb=wb, bk=bk: e.matmul(
                                        PS[bk][:, 0:PW], lhsT=wb[:, kc, ci * 128:(ci + 1) * 128],
                                        rhs=hTt[:, kc, hf * PW:(hf + 1) * PW], start=(kc == 0), stop=(kc == KC - 1))
                                        for kc in range(KC)]
                                    S.group('pe', fns, reads=[wB, hTB], writes=[PSB[bk]])
                                    dsta = stg[:, ci, hf * PW:(hf + 1) * PW]
                                    if gate:
                                        S.op('act', lambda e, dsta=dsta, bk=bk: e.activation(
                                            out=dsta, in_=PS[bk][:, 0:PW], func=AF.Sigmoid), reads=[PSB[bk]], writes=[sgB])
                                    elif nm == "qa":
                                        S.op('act', lambda e, dsta=dsta, bk=bk: e.activation(
                                            out=dsta, in_=PS[bk][:, 0:PW], func=AF.Copy, scale=qs), reads=[PSB[bk]],
                                            writes=[sgB])
                                    else:
                                        S.op('dve', lambda e, dsta=dsta, bk=bk: e.tensor_copy(out=dsta, in_=PS[bk][:, 0:PW]),
                                             reads=[PSB[bk]], writes=[sgB])
                            if nm == "qa":
                                dd = qT_d.ap()[s0:s0 + sw, to0:to0 + TS]
                            elif nm == "ka":
                                dd = kT_d.ap()[s0:s0 + sw, t0:t0 + TS]
                            elif nm == "ga":
                                dd = gT_d.ap()[s0:s0 + sw, to0:to0 + TS]
                            else:
                                dd = gT_d.ap()[D + s0:D + s0 + sw, to0:to0 + TS]
                            S.dma('sp', dd.rearrange("(k p) t -> p k t", p=128), stg[:, 0:nch, :], reads=[sgB])
                        else:
                            rope = kind == "rope"
                            if rope:
                                stg, sgB = stb[nst % 2], stbB[nst % 2]
                                nst += 1
                            else:
                                stg, sgB = stt[nst % 2], sttB[nst % 2]
                                nst += 1
                            pending = []
                            for tb in range(tb_lo, NTB):
                                bk = nbk % 6
                                nbk += 1
                                fns = [lambda e, kc=kc, tb=tb, wb=wb, bk=bk, sw=sw: e.matmul(
                                    PS[bk][:, 0:sw], lhsT=hTt[:, kc, tb * 128:(tb + 1) * 128], rhs=wb[:, kc, 0:sw],
                                    start=(kc == 0), stop=(kc == KC - 1)) for kc in range(KC)]
                                S.group('pe', fns, reads=[wB, hTB], writes=[PSB[bk]])
                                if not rope:
                                    dsta = stg[:, tb, 0:sw]
                                    if tb % 2 == 0:
                                        S.op('act', lambda e, dsta=dsta, bk=bk, sw=sw: e.activation(
                                            out=dsta, in_=PS[bk][:, 0:sw], func=AF.Copy), reads=[PSB[bk]], writes=[sgB])
                                    else:
                                        S.op('dve', lambda e, dsta=dsta, bk=bk, sw=sw: e.tensor_copy(
                                            out=dsta, in_=PS[bk][:, 0:sw]), reads=[PSB[bk]], writes=[sgB])
                                    continue
                                gtb = (t0 // 128) + tb
                                if nm == "qb":
                                    ct, stn = cosQ[:, gtb - c.NTO, :], sinQ[:, gtb - c.NTO, :]
                                else:
                                    ct, stn = cosT[:, gtb, :], sinT[:, gtb, :]
                                r_, rB_ = rp[nrp % 2], rpB[nrp % 2]
                                a0, a0B = ra[0], raB[0]
                                a1, a1B = ra[1], raB[1]
                                nrp += 1
                                xv = PS[bk][:, 0:sw].rearrange("p (h two d) -> p h two d", two=2, d=64)
                                rv = r_[:, 0:sw].rearrange("p (h two d) -> p h two d", two=2, d=64)
                                a0v = a0[:, 0:nch * 64].rearrange("p (h d) -> p h d", d=64)
                                a1v = a1[:, 0:nch * 64].rearrange("p (h d) -> p h d", d=64)
                                for hh in range(nch):
                                    pass
                                cb = bass.AP(ct.tensor, ct.offset, [list(ct.ap[0]), [0, nch], list(ct.ap[-1])])
                                sbp = bass.AP(stn.tensor, stn.offset, [list(stn.ap[0]), [0, nch], list(stn.ap[-1])])
                                x1, x2 = xv[:, :, 0, :], xv[:, :, 1, :]
                                S.op('dve', lambda e, a0v=a0v, x1=x1, cb=cb: e.tensor_tensor(out=a0v, in0=x1, in1=cb, op=ALU.mult),
                                     reads=[PSB[bk], CONST], writes=[a0B])
                                S.op('dve', lambda e, a1v=a1v, x2=x2, sbp=sbp: e.tensor_tensor(out=a1v, in0=x2, in1=sbp, op=ALU.mult),
                                     reads=[PSB[bk], CONST], writes=[a1B])
                                S.op('dve', lambda e, rv=rv, a0v=a0v, a1v=a1v: e.tensor_tensor(
                                    out=rv[:, :, 0, :], in0=a0v, in1=a1v, op=ALU.subtract), reads=[a0B, a1B], writes=[rB_])
                                S.op('dve', lambda e, a0v=a0v, x2=x2, cb=cb: e.tensor_tensor(out=a0v, in0=x2, in1=cb, op=ALU.mult),
                                     reads=[PSB[bk], CONST], writes=[a0B])
                                S.op('dve', lambda e, a1v=a1v, x1=x1, sbp=sbp: e.tensor_tensor(out=a1v, in0=x1, in1=sbp, op=ALU.mult),
                                     reads=[PSB[bk], CONST], writes=[a1B])
                                S.op('dve', lambda e, rv=rv, a0v=a0v, a1v=a1v: e.tensor_tensor(
                                    out=rv[:, :, 1, :], in0=a0v, in1=a1v, op=ALU.add), reads=[a0B, a1B], writes=[rB_])
                                for f_ in pending:
                                    f_()
                                bk2 = 6 + (nrp % 2)

                                def do_tr(r_=r_, rB_=rB_, bk2=bk2, tb=tb, stg=stg, sgB=sgB, nch=nch):
                                    psb = PS[bk2].bitcast(BF16)
                                    fns = [lambda e, hh=hh: e.transpose(
                                        psb[:, hh * 128:(hh + 1) * 128], r_[:, hh * 128:(hh + 1) * 128], identb[:, :])
                                        for hh in range(nch)]
                                    S.group('pe', fns, reads=[rB_, CONST], writes=[PSB[bk2]])
                                    S.op('act', lambda e: e.activation(
                                        out=stg[:, 0:nch, tb * 128:(tb + 1) * 128],
                                        in_=psb[:, 0:nch * 128].rearrange("p (a b) -> p a b", a=nch), func=AF.Copy),
                                        reads=[PSB[bk2]], writes=[sgB])
                                pending = [do_tr]
                            for f_ in pending:
                                f_()
                            c_lo = tb_lo * 128
                            if rope:
                                if nm == "qb":
                                    dd = qdT_d.ap()[s0:s0 + sw, to0 + c_lo:to0 + TS]
                                else:
                                    dd = kdT_d.ap()[s0:s0 + sw, t0 + c_lo:t0 + TS]
                                S.dma('sp', dd.rearrange("(k p) t -> p k t", p=128), stg[:, 0:nch, c_lo:], reads=[sgB])
                            else:
                                dd = (v_d if nm == "va" else vd_d).ap()[t0 + c_lo:t0 + TS, s0:s0 + sw]
                                S.dma('sp', dd.rearrange("(t p) f -> p t f", p=128), stg[:, tb_lo:, 0:sw], reads=[sgB])
            S.barrier(PSB + [CONST])

        TQ = c.TQ
        NQT = T // TQ
        NDB = TQ // 128
        with contextlib.ExitStack() as ph:
            kTh = [sb(f"kTh{i}", [128, L], BF16, ph) for i in range(2)]
            kTB = [Buf(f"kTh{i}") for i in range(2)]
            vh = [sb(f"vh{i}", [128, c.NT, 128], BF16, ph) for i in range(2)]
            vB = [Buf(f"vh{i}") for i in range(2)]
            qTh = [sb(f"qTh{i}", [128, T], BF16, ph) for i in range(2)]
            qB = [Buf(f"qTh{i}") for i in range(2)]
            sbm = sb("sbm", [128, NDB, TQ], BF16, ph)
            NE, NL, NA = 3, 4, 3
            e_ = [sb(f"e{i}", [128, TQ], F32, ph) for i in range(NE)]
            eB = [Buf(f"e{i}") for i in range(NE)]
            lp = [sb(f"lp{i}", [128, TQ], BF16, ph) for i in range(NL)]
            lpB = [Buf(f"lp{i}") for i in range(NL)]
            lacc = [sb(f"lacc{i}", [128, TQ], BF16, ph) for i in range(3)]
            laccB = [Buf(f"lacc{i}") for i in range(3)]
            am = [sb(f"am{i}", [128, TQ], BF16, ph) for i in range(NA)]
            amB = [Buf(f"am{i}") for i in range(NA)]
            yst = [sb(f"yst{i}", [128, TQ], BF16, ph) for i in range(2)]
            ystB = [Buf(f"yst{i}") for i in range(2)]
            S.dma('sp', sbm[:, :, :], k_sbmask.ap(), writes=[CONST])

            gsteps = []
            if GATES_IN_P3:
                GTS, GPW = c.TS, c.PW
                GNH = GTS // GPW
                GSW = min(256, D)
                hTg = sb("hTg", [128, KC, GTS], BF16, ph)
                hTgB = Buf("hTg")
                wg = [sb(f"wg{i}", [128, KC, GSW], BF16, ph) for i in range(3)]
                wgB = [Buf(f"wg{i}") for i in range(3)]
                gst = [sb(f"gst{i}", [128, GSW // 128, GTS], F32, ph) for i in range(2)]
                gstB = [Buf(f"gst{i}") for i in range(2)]
                gex = [sb(f"gex{i}", [128, GPW], F32, ph) for i in range(2)]
                gexB = [Buf(f"gex{i}") for i in range(2)]
                gslabs = [(st_, sg, s0_) for st_ in range(T // GTS) for sg in range(2) for s0_ in range(0, D, GSW)]
                gd = [0]

                def gate_dma(upto):
                    while gd[0] <= min(upto, len(gslabs) - 1):
                        i = gd[0]
                        st_, sg, s0_ = gslabs[i]
                        dst, src = wload(wg[i % 3][:, :, :], w_in.ap(), 0, D, (o_ga if sg == 0 else o_gb) + s0_, GSW)
                        S.dma('pool', dst, src, writes=[wgB[i % 3]])
                        gd[0] += 1

                gcnt = [0]
                for gi_, (st_, sg, s0_) in enumerate(gslabs):
                    wgi, wgBi = wg[gi_ % 3], wgB[gi_ % 3]
                    stg_, stgB_ = gst[gi_ % 2], gstB[gi_ % 2]
                    for ci in range(GSW // 128):
                        for hf in range(GNH):
                            k_ = gcnt[0]
                            gcnt[0] += 1
                            gbk = 4 + (k_ % 2)
                            ex_, exB_ = gex[k_ % 2], gexB[k_ % 2]

                            def first(gi_=gi_, st_=st_, s0_=s0_, sg=sg, ci=ci, hf=hf, gbk=gbk, wgBi=wgBi):
                                gate_dma(gi_ + 2)
                                if ci == 0 and hf == 0 and gi_ % 3 == 2:
                                    conv_run(1)
                                if sg == 0 and s0_ == 0 and ci == 0 and hf == 0:
                                    for k4 in range(0, KC, 8):
                                        k4e = min(KC, k4 + 8)
                                        S.dma('sp', hTg[:, k4:k4e, :],
                                              hT_d.ap()[k4 * 128:k4e * 128, T + st_ * GTS:T + (st_ + 1) * GTS].rearrange(
                                                  "(k p) t -> p k t", p=128), writes=[hTgB])
                                S.begin('pe', reads=[wgBi, hTgB], writes=[PSB[gbk]])
                            gsteps.append(first)
                            for kc in range(KC):
                                fn = (lambda e, kc=kc, ci=ci, hf=hf, gbk=gbk, wgi=wgi: e.matmul(
                                    PS[gbk][:, 0:GPW], lhsT=wgi[:, kc, ci * 128:(ci + 1) * 128],
                                    rhs=hTg[:, kc, hf * GPW:(hf + 1) * GPW], start=(kc == 0), stop=(kc == KC - 1)))
                                if kc < KC - 1:
                                    gsteps.append(lambda fn=fn: S.raw('pe', fn))
                                    continue

                                def last(fn=fn, ci=ci, hf=hf, gbk=gbk, wgBi=wgBi, ex_=ex_, exB_=exB_, stg_=stg_, stgB_=stgB_,
                                         st_=st_, sg=sg, s0_=s0_):
                                    S.group('pe', [fn], reads=[wgBi, hTgB], writes=[PSB[gbk]])
                                    S.op('act', lambda e: e.activation(out=ex_[:, :], in_=PS[gbk][:, 0:GPW], func=AF.Exp,
                                                                       scale=-1.0), reads=[PSB[gbk]], writes=[exB_])
                                    S.op('dve', lambda e: e.tensor_scalar(out=ex_[:, :], in0=ex_[:, :], scalar1=1.0,
                                                                          scalar2=None, op0=ALU.add),
                                         reads=[exB_], writes=[exB_])
                                    S.op('dve', lambda e: e.reciprocal(out=stg_[:, ci, hf * GPW:(hf + 1) * GPW], in_=ex_[:, :]),
                                         reads=[exB_], writes=[stgB_])
                                    if ci == GSW // 128 - 1 and hf == GNH - 1:
                                        dd = gT_d.ap()[sg * D + s0_:sg * D + s0_ + GSW, st_ * GTS:(st_ + 1) * GTS]
                                        S.dma('sp', dd.rearrange("(k p) t -> p k t", p=128), stg_[:, :, :], reads=[stgB_])
                                gsteps.append(last)
            gpos = [0]

            def gate_run(n):
                for f_ in gsteps[gpos[0]:gpos[0] + n]:
                    f_()
                gpos[0] = min(len(gsteps), gpos[0] + n)

            def load_head(h):
                S.dma('sp', kTh[h % 2][:, :], kT_d.ap()[h * 128:(h + 1) * 128, :], writes=[kTB[h % 2]])
                S.dma('sp', qTh[h % 2][:, :], qT_d.ap()[h * 128:(h + 1) * 128, :], writes=[qB[h % 2]])
                S.dma('sp', vh[h % 2][:, :, :], v_d.ap()[:, h * 128:(h + 1) * 128].rearrange("(t p) d -> p t d", p=128),
                      writes=[vB[h % 2]])

            blocks = []
            nq = 0
            for h in range(c.NH_SB):
                for j in range(NQT):
                    kbs = list(range(c.NTO + (j + 1) * NDB - 1, -1, -1))
                    for bi, kb in enumerate(kbs):
                        blocks.append(dict(h=h, j=j, kb=kb, bi=bi, n=len(kbs), nq=nq, idx=len(blocks)))
                    nq += 1
            NBK = len(blocks)
            st = {}

            def s123(b):
                h, j, kb, i = b['h'], b['j'], b['kb'], b['idx']
                kT, kB_ = kTh[h % 2], kTB[h % 2]
                qT, qB_ = qTh[h % 2], qB[h % 2]
                qs_ = qT[:, j * TQ:(j + 1) * TQ]
                kblk = kT[:, kb * 128:(kb + 1) * 128]
                ctx = kb < c.NTO
                dg = kb - (c.NTO + j * NDB)
                bias = cbias[:, 0:1] if ctx else c_zero[:, 0:1]
                zb = i % 2
                ee, eeB = e_[i % NE], eB[i % NE]
                l_, lB_ = lp[i % NL], lpB[i % NL]
                S.op('pe', lambda e: e.matmul(PS[zb][:, 0:TQ], lhsT=kblk, rhs=qs_, start=True, stop=True),
                     reads=[kB_, qB_], writes=[PSB[zb]])
                S.op('act', lambda e: e.activation(out=ee[:, :], in_=PS[zb][:, 0:TQ], func=AF.Exp, bias=bias, scale=1.0),
                     reads=[PSB[zb], CONST], writes=[eeB])
                S.op('act', lambda e: e.activation(out=l_[:, :], in_=ee[:, :], func=AF.Ln, bias=c_one[:, 0:1], scale=1.0),
                     reads=[eeB, CONST], writes=[lB_])
                if dg >= 0:
                    S.op('dve', lambda e: e.tensor_tensor(out=l_[:, :], in0=l_[:, :], in1=sbm[:, dg, :], op=ALU.mult),
                         reads=[lB_, CONST], writes=[lB_])
                b.update(qs_=qs_, kblk=kblk, bias=bias, dg=dg, l_=l_, lB_=lB_, kB_=kB_, qB_=qB_)

            def s45(b):
                i = b['idx']
                bb = 2 + (i % 2)
                a_, aB_ = am[i % NA], amB[i % NA]
                l_, lB_, kblk, qs_ = b['l_'], b['lB_'], b['kblk'], b['qs_']
                la = None if b['bi'] == 0 else st['la']
                fns = [lambda e: e.matmul(PS[bb][:, 0:TQ], lhsT=kblk, rhs=qs_, start=True, stop=False),
                       lambda e: e.matmul(PS[bb][:, 0:TQ], lhsT=negtri[:, :], rhs=l_[:, :], start=False, stop=(la is None))]
                rds = [b['kB_'], b['qB_'], lB_, CONST]
                if la is not None:
                    la_t, la_B = la
                    fns.append(lambda e: e.matmul(PS[bb][:, 0:TQ], lhsT=negones[:, :], rhs=la_t[:, :], start=False, stop=True))
                    rds.append(la_B)
                S.group('pe', fns, reads=rds, writes=[PSB[bb]])
                bias, dg = b['bias'], b['dg']
                S.op('act', lambda e: e.activation(out=a_[:, :], in_=PS[bb][:, 0:TQ], func=AF.Exp, bias=bias, scale=1.0),
                     reads=[PSB[bb], CONST], writes=[aB_])
                if dg >= 0:
                    S.op('dve', lambda e: e.tensor_tensor(out=a_[:, :], in0=a_[:, :], in1=sbm[:, dg, :], op=ALU.mult),
                         reads=[aB_, CONST], writes=[aB_])
                b.update(a_=a_, aB_=aB_)
                if b['bi'] < b['n'] - 1:
                    if la is None:
                        st['la'] = (l_, lB_)
                    else:
                        la_t, la_B = la
                        k3 = st.get('k3', 0)
                        st['k3'] = k3 + 1
                        ln_, lnB = lacc[k3 % 3], laccB[k3 % 3]
                        S.op('pool', lambda e: e.tensor_tensor(out=ln_[:, :], in0=la_t[:, :], in1=l_[:, :], op=ALU.add),
                             reads=[la_B, lB_], writes=[lnB])
                        st['la'] = (ln_, lnB)

            def s6(b):
                h, j, kb = b['h'], b['j'], b['kb']
                ybk = 6 + (b['nq'] % 2)
                vv, vB_ = vh[h % 2], vB[h % 2]
                a_, aB_ = b['a_'], b['aB_']
                first, last = b['bi'] == 0, b['bi'] == b['n'] - 1
                S.op('pe', lambda e: e.matmul(PS[ybk][:, 0:TQ], lhsT=vv[:, kb, :], rhs=a_[:, :], start=first, stop=last),
                     reads=[vB_, aB_], writes=[PSB[ybk]])
                if last:
                    ys, ysB = yst[b['nq'] % 2], ystB[b['nq'] % 2]
                    S.op('dve', lambda e: e.tensor_copy(out=ys[:, :], in_=PS[ybk][:, 0:TQ]), reads=[PSB[ybk]], writes=[ysB])
                    S.dma('sp', yaT_d.ap()[h * 128:(h + 1) * 128, j * TQ:(j + 1) * TQ], ys[:, :], reads=[ysB])

            first3 = {}
            for b in blocks:
                first3.setdefault(b['h'], b['idx'])
            load_head(0)
            gper = -(-len(gsteps) // max(1, NBK))
            for t in range(NBK + 2):
                if t < NBK:
                    s123(blocks[t])
                if 0 <= t - 1 < NBK:
                    s45(blocks[t - 1])
                if 0 <= t - 2 < NBK:
                    s6(blocks[t - 2])
                for h_, f_ in first3.items():
                    if t == f_ + 2 and h_ + 1 < c.NH_SB:
                        load_head(h_ + 1)
                gate_run(gper)
            gate_run(len(gsteps))
            S.barrier(PSB + [CONST])

        with contextlib.ExitStack() as ph:
            dm = sb("dm", [128, NTAP, 128], BF16, ph)
            dmc = sb("dmc", [128, NTAP, 128], BF16, ph)
            S.dma('sp', dm[:, :, :], k_dmask.ap(), writes=[CONST])
            S.dma('sp', dmc[:, :, :], k_dmaskc.ap(), writes=[CONST])
            kd = [[sb(f"kd{i}_{g}", [128, L], BF16, ph) for g in range(c.NG)] for i in range(2)]
            qd = [[sb(f"qd{i}_{g}", [128, T], BF16, ph) for g in range(c.NG)] for i in range(2)]
            vdt = [[sb(f"vd{i}_{g}", [128, c.NT, 132], BF16, ph) for g in range(c.NG)] for i in range(2)]
            hdB = [Buf(f"hd{i}") for i in range(2)]
            NPE = 4
            pe_ = [sb(f"pe{i}", [128, 512], F32, ph) for i in range(NPE)]
            peB = [Buf(f"pe{i}") for i in range(NPE)]
            pm = [sb(f"pm{i}", [128, 512], BF16, ph) for i in range(NPE)]
            pmB = [[Buf(f"pm{i}_{q}") for q in range(4)] for i in range(NPE)]
            rden = [sb(f"rden{i}", [128, 2], F32, ph) for i in range(2)]
            rdB = [Buf(f"rden{i}") for i in range(2)]
            yq = [sb(f"yq{i}", [128, 128], BF16, ph) for i in range(2)]
            yqB = [Buf(f"yq{i}") for i in range(2)]
            ybst = [sb(f"ybst{i}", [128, T], BF16, ph) for i in range(2)]
            ybB = [Buf(f"ybst{i}") for i in range(2)]
            wsl4 = [sb(f"w4_{i}", [128, KC, SW0], BF16, ph) for i in range(3)]
            wsl4B = [Buf(f"w4_{i}") for i in range(3)]
            rest4 = list(range(nfirst, nsl0))

            def load_dhead(hd):
                i2 = hd % 2
                for g in range(c.NG):
                    row = g * c.D_DIL + hd * 128
                    lo_t = max(0, T - c.groups[g][0])
                    lo_b = lo_t // 128
                    S.dma('sp', kd[i2][g][:, lo_t:], kdT_d.ap()[row:row + 128, lo_t:], writes=[hdB[i2]])
                    S.dma('sp', qd[i2][g][:, :], qdT_d.ap()[row:row + 128, :], writes=[hdB[i2]])
                    S.dma('sp', vdt[i2][g][:, lo_b:, 0:128],
                          vd_d.ap()[lo_t:, row:row + 128].rearrange("(t p) d -> p t d", p=128), writes=[hdB[i2]])
                    S.op('pool', lambda e, i2=i2, g=g: e.memset(vdt[i2][g][:, :, 128:129], 1.0), writes=[hdB[i2]])

            tgs = []
            nqb = 0
            for hd in range(c.NH_DIL):
                for qi in range(c.NTO):
                    for t4 in range(0, NTAP, 4):
                        tgs.append(dict(hd=hd, qi=qi, t4=t4, grp=c.taps[t4:t4 + 4], nqb=nqb, idx=len(tgs)))
                    nqb += 1
            NTG = len(tgs)

            def sA(b):
                hd, qi, t4, grp, i = b['hd'], b['qi'], b['t4'], b['grp'], b['idx']
                i2 = hd % 2
                sbk = i % 3
                p_e, p_eB = pe_[i % NPE], peB[i % NPE]
                p_m, p_mB = pm[i % NPE], pmB[i % NPE]
                fns = []
                for ti, (g, dl) in enumerate(grp):
                    kb = c.NTO + qi - dl
                    fns.append(lambda e, ti=ti, g=g, kb=kb: e.matmul(
                        PS[sbk][:, ti * 128:(ti + 1) * 128], lhsT=kd[i2][g][:, kb * 128:(kb + 1) * 128],
                        rhs=qd[i2][g][:, qi * 128:(qi + 1) * 128], start=True, stop=True))
                S.group('pe', fns, reads=[hdB[i2]], writes=[PSB[sbk]])
                w = len(grp) * 128
                S.op('act', lambda e: e.activation(out=p_e[:, 0:w], in_=PS[sbk][:, 0:w], func=AF.Exp),
                     reads=[PSB[sbk]], writes=[p_eB])
                for ti, (g, dl) in enumerate(grp):
                    kb = c.NTO + qi - dl
                    mk = dmc if kb < c.NTO else dm
                    en = 'dve'
                    S.op(en, lambda e, ti=ti, mk=mk: e.tensor_tensor(
                        out=p_m[:, ti * 128:(ti + 1) * 128], in0=p_e[:, ti * 128:(ti + 1) * 128],
                        in1=mk[:, t4 + ti, :], op=ALU.mult), reads=[p_eB, CONST], writes=[p_mB[ti]])
                b.update(p_m=p_m, p_mB=p_mB)

            def sB(b):
                hd, qi, t4, grp, nq_ = b['hd'], b['qi'], b['t4'], b['grp'], b['nqb']
                i2 = hd % 2
                obk = 6 + (nq_ % 2)
                p_m, p_mB = b['p_m'], b['p_mB']
                fns = []
                for ti, (g, dl) in enumerate(grp):
                    kb = c.NTO + qi - dl
                    tix = t4 + ti
                    fns.append(lambda e, ti=ti, g=g, kb=kb, tix=tix: e.matmul(
                        PS[obk][:, 0:129], lhsT=p_m[:, ti * 128:(ti + 1) * 128], rhs=vdt[i2][g][:, kb, 0:129],
                        start=(tix == 0), stop=(tix == NTAP - 1)))
                S.group('pe', fns, reads=p_mB + [hdB[i2]], writes=[PSB[obk]])
                if t4 + 4 >= NTAP:
                    y_, yB_ = yq[nq_ % 2], yqB[nq_ % 2]
                    rd, rdB_ = rden[nq_ % 2], rdB[nq_ % 2]
                    S.op('dve', lambda e: e.reciprocal(out=rd[:, 0:1], in_=PS[obk][:, 128:129]),
                         reads=[PSB[obk]], writes=[rdB_])
                    S.op('dve', lambda e: e.tensor_scalar(out=y_[:, :], in0=PS[obk][:, 0:128], scalar1=rd[:, 0:1],
                                                          scalar2=None, op0=ALU.mult),
                         reads=[PSB[obk], rdB_], writes=[yB_])
                    tbk = 4 + (nq_ % 2)
                    psb = PS[tbk].bitcast(BF16)
                    S.op('pe', lambda e: e.transpose(psb[:, 0:128], y_[:, :], identb[:, :]),
                         reads=[yB_, CONST], writes=[PSB[tbk]])
                    S.op('act', lambda e: e.activation(out=ybst[i2][:, qi * 128:(qi + 1) * 128], in_=psb[:, 0:128],
                                                       func=AF.Copy), reads=[PSB[tbk]], writes=[ybB[i2]])
                    if qi == c.NTO - 1:
                        S.dma('sp', ybT_d.ap()[hd * 128:(hd + 1) * 128, :], ybst[i2][:, :], reads=[ybB[i2]])

            first4 = {}
            for b in tgs:
                first4.setdefault(b['hd'], b['idx'])
            load_dhead(0)
            for t in range(NTG + 2):
                if t < NTG:
                    sA(tgs[t])
                if 0 <= t - 2 < NTG:
                    sB(tgs[t - 2])
                for h_, f_ in first4.items():
                    if t == f_ + 2 and h_ + 1 < c.NH_DIL:
                        load_dhead(h_ + 1)
                if t < NTG:
                    lo_ = (t * len(rest4)) // NTG
                    hi_ = ((t + 1) * len(rest4)) // NTG
                    for sl in rest4[lo_:hi_]:
                        mod_slab(sl, wsl4, wsl4B, 3, nsl0 - 1)
            mod_finish(2 * KC, 6 * KC, MB["mod2"], A2, nw2, SC2, 3)
            if "mod_d" in debug_outs:
                S.dma('sp', mod_d.ap(), modsb[:, :], reads=[MB["mod1"], MB["mod2"]])
            S.barrier(PSB + [CONST])

        KA = c.D_SB // 128
        KB = c.D_DIL // 128
        TS = min(512, T)
        PW = min(512, TS)
        NHF = TS // PW
        with contextlib.ExitStack() as ph:
            yaT = sb("yaT", [128, KA, TS], BF16, ph)
            ybT = sb("ybT", [128, KB, TS], BF16, ph)
            yB5 = Buf("y5")
            mT = sb("mT", [128, KC, TS], BF16, ph)
            mBs = [Buf(f"mT{i}") for i in range(KC)]
            SW5 = 512
            wa = [sb(f"wa{i}", [128, KA, SW5], BF16, ph) for i in range(2)]
            wbb = [sb(f"wb{i}", [128, KB, SW5], BF16, ph) for i in range(2)]
            w5B = [Buf(f"w5{i}") for i in range(2)]
            wo = [sb(f"wo{i}", [128, KC, SW5], BF16, ph) for i in range(2)]
            woB = [Buf(f"wo{i}") for i in range(2)]
            gt = [sb(f"gt{i}", [128, 2, PW], F32, ph) for i in range(2)]
            gtB = [Buf(f"gt{i}") for i in range(2)]
            t5 = [sb(f"t5{i}", [128, PW], F32, ph) for i in range(2)]
            t5B = [Buf(f"t5{i}") for i in range(2)]
            xr = [sb(f"xr{i}", [128, PW], F32, ph) for i in range(2)]
            xrB = [Buf(f"xr{i}") for i in range(2)]
            x1s = [sb(f"x1s{i}", [128, PW], F32, ph) for i in range(2)]
            x1B = [Buf(f"x1s{i}") for i in range(2)]
            nw5 = 0
            ng = 0
            nbk = 0
            sw5 = min(SW5, D)
            wl5 = []
            for st in range(T // TS):
                wl5 += [('ab', s0) for s0 in range(0, D, sw5)] + [('o', s0) for s0 in range(0, D, sw5)]
            cnt5 = dict(ab=0, o=0, g=0, dab=0, do=0, d=0)

            def ensure5(upto):
                while cnt5['d'] <= min(upto, len(wl5) - 1):
                    kind5, s0_ = wl5[cnt5['d']]
                    if kind5 == 'ab':
                        i5_ = cnt5['dab'] % 2
                        cnt5['dab'] += 1
                        dst, src = wload(wa[i5_][:, :, 0:sw5], w_o_sb.ap(), 0, c.D_SB, s0_, sw5)
                        S.dma('pool', dst, src, writes=[w5B[i5_]])
                        dst, src = wload(wbb[i5_][:, :, 0:sw5], w_o_dil.ap(), 0, c.D_DIL, s0_, sw5)
                        S.dma('pool', dst, src, writes=[w5B[i5_]])
                    else:
                        io_ = cnt5['do'] % 2
                        cnt5['do'] += 1
                        dst, src = wload(wo[io_][:, :, 0:sw5], w_out.ap(), 0, D, s0_, sw5)
                        S.dma('pool', dst, src, writes=[woB[io_]])
                    cnt5['d'] += 1
                    conv_run(1)

            ensure5(0)
            for st in range(T // TS):
                to0 = st * TS
                S.dma('sp', yaT[:, :, :], yaT_d.ap()[:, to0:to0 + TS].rearrange("(k p) t -> p k t", p=128), writes=[yB5])
                S.dma('sp', ybT[:, :, :], ybT_d.ap()[:, to0:to0 + TS].rearrange("(k p) t -> p k t", p=128), writes=[yB5])
                sw = min(SW5, D)
                for s0 in range(0, D, sw):
                    i5 = cnt5['ab'] % 2
                    cnt5['ab'] += 1
                    ensure5(cnt5['g'] + 1)
                    cnt5['g'] += 1
                    for ci in range(sw // 128):
                        fch = (s0 // 128) + ci
                        for hf in range(NHF):
                            ig = ng % 2
                            ng += 1
                            tk0 = to0 + hf * PW
                            S.dma('sp', gt[ig][:, 0, :], gT_d.ap()[fch * 128:(fch + 1) * 128, tk0:tk0 + PW], writes=[gtB[ig]])
                            S.dma('sp', gt[ig][:, 1, :], gT_d.ap()[D + fch * 128:D + (fch + 1) * 128, tk0:tk0 + PW],
                                  writes=[gtB[ig]])
                            ba = nbk % 6
                            bbk = (nbk + 1) % 6
                            nbk += 2
                            fns = [lambda e, kc=kc, ci=ci, hf=hf, i5=i5, ba=ba: e.matmul(
                                PS[ba][:, 0:PW], lhsT=wa[i5][:, kc, ci * 128:(ci + 1) * 128],
                                rhs=yaT[:, kc, hf * PW:(hf + 1) * PW], start=(kc == 0), stop=(kc == KA - 1)) for kc in range(KA)]
                            S.group('pe', fns, reads=[w5B[i5], yB5], writes=[PSB[ba]])
                            fns = [lambda e, kc=kc, ci=ci, hf=hf, i5=i5, bbk=bbk: e.matmul(
                                PS[bbk][:, 0:PW], lhsT=wbb[i5][:, kc, ci * 128:(ci + 1) * 128],
                                rhs=ybT[:, kc, hf * PW:(hf + 1) * PW], start=(kc == 0), stop=(kc == KB - 1)) for kc in range(KB)]
                            S.group('pe', fns, reads=[w5B[i5], yB5], writes=[PSB[bbk]])
                            ta, taB = t5[0], t5B[0]
                            tb_, tbB = t5[1], t5B[1]
                            S.op('dve', lambda e, ta=ta, ba=ba, ig=ig: e.tensor_tensor(out=ta[:, :], in0=PS[ba][:, 0:PW],
                                                                                       in1=gt[ig][:, 0, :], op=ALU.mult),
                                 reads=[PSB[ba], gtB[ig]], writes=[taB])
                            S.op('dve', lambda e, tb_=tb_, bbk=bbk, ig=ig: e.tensor_tensor(out=tb_[:, :], in0=PS[bbk][:, 0:PW],
                                                                                           in1=gt[ig][:, 1, :], op=ALU.mult),
                                 reads=[PSB[bbk], gtB[ig]], writes=[tbB])
                            S.op('dve', lambda e, ta=ta, tb_=tb_, fch=fch, hf=hf: e.tensor_tensor(
                                out=mT[:, fch, hf * PW:(hf + 1) * PW], in0=ta[:, :], in1=tb_[:, :], op=ALU.add),
                                reads=[taB, tbB], writes=[mBs[fch]])
                for s0 in range(0, D, sw):
                    io = cnt5['o'] % 2
                    cnt5['o'] += 1
                    ensure5(cnt5['g'] + 1)
                    cnt5['g'] += 1
                    for ci in range(sw // 128):
                        fch = (s0 // 128) + ci
                        for hf in range(NHF):
                            ix = ng % 2
                            ng += 1
                            tk0 = to0 + hf * PW
                            S.dma('sp', xr[ix][:, :], xT_d.ap()[fch * 128:(fch + 1) * 128, tk0:tk0 + PW], writes=[xrB[ix]])
                            bk = nbk % 6
                            nbk += 1
                            fns = [lambda e, kc=kc, ci=ci, hf=hf, io=io, bk=bk: e.matmul(
                                PS[bk][:, 0:PW], lhsT=wo[io][:, kc, ci * 128:(ci + 1) * 128],
                                rhs=mT[:, kc, hf * PW:(hf + 1) * PW], start=(kc == 0), stop=(kc == KC - 1)) for kc in range(KC)]
                            S.group('pe', fns, reads=[woB[io]] + mBs, writes=[PSB[bk]])
                            S.op('dve', lambda e, ix=ix, bk=bk, fch=fch: e.scalar_tensor_tensor(
                                out=x1s[ix][:, :], in0=PS[bk][:, 0:PW], scalar=modsb[:, G1 + fch:G1 + fch + 1],
                                in1=xr[ix][:, :], op0=ALU.mult, op1=ALU.add), reads=[PSB[bk], xrB[ix]], writes=[x1B[ix]])
                            S.dma('sp', x1T_d.ap()[fch * 128:(fch + 1) * 128, tk0:tk0 + PW], x1s[ix][:, :], reads=[x1B[ix]])
            conv_run(len(conv_steps))
            S.barrier(PSB + [CONST])

        TF = c.TF
        NFB = TF // 128
        with contextlib.ExitStack() as ph:
            acc = sb("acc", [128, KC, TF], F32, ph)
            accB = [Buf(f"acc{i}") for i in range(KC)]
            h2 = sb("h2", [128, KC, TF], BF16, ph)
            h2B = Buf("h2")
            NFC = SWF // 128
            w1 = [sb(f"w1_{i}", [128, KC, SWF], BF16, ph) for i in range(2)]
            w1B = [Buf(f"w1_{i}") for i in range(2)]
            w2 = [sb(f"w2f_{i}", [128, NFC, D], BF16, ph) for i in range(2)]
            w2B = [Buf(f"w2f_{i}") for i in range(2)]
            ut = [sb(f"ut{i}", [128, NFC, TF], BF16, ph) for i in range(2)]
            utB = [Buf(f"ut{i}") for i in range(2)]
            sqs = [sb(f"fsq{i}", [128, TF], F32, ph) for i in range(2)]
            sqB = [Buf(f"fsq{i}") for i in range(2)]
            tmps = [sb(f"ftm{i}", [128, TF], F32, ph) for i in range(2)]
            tmpB = [Buf(f"ftm{i}") for i in range(2)]
            rs = sb("frs", [128, TF], F32, ph)
            rstd = sb("frstd", [128, TF], F32, ph)
            rB = Buf("frs")
            ot = [sb(f"ot{i}", [128, D], F32, ph) for i in range(1)]
            otB = [Buf(f"ot{i}") for i in range(1)]
            nev = 0
            gl = [(ft, fg) for ft in range(T // TF) for fg in range(NGF)]
            wslot = {}

            def load_w(gi):
                iw = gi % 2
                ft, fg = gl[gi]
                S.dma('pool', w1[iw][:, :, :], w1b_d.ap()[fg], writes=[w1B[iw]])
                S.dma('pool', w2[iw][:, :, :],
                      w2b_d.ap()[fg * SWF:(fg + 1) * SWF, :].rearrange("(k p) f -> p k f", p=128), writes=[w2B[iw]])

            def ffn1_pieces(gi):
                iw = gi % 2
                u_, uB_ = ut[iw], utB[iw]
                steps = []
                for ci in range(NFC):
                    bk = (gi * NFC + ci) % 3

                    def first(ci=ci, bk=bk):
                        S.begin('pe', reads=[w1B[iw], h2B], writes=[PSB[bk]])
                    steps.append(first)
                    for kc in range(KC):
                        fn = (lambda e, kc=kc, ci=ci, bk=bk: e.matmul(
                            PS[bk][:, 0:TF], lhsT=w1[iw][:, kc, ci * 128:(ci + 1) * 128], rhs=h2[:, kc, :],
                            start=(kc == 0), stop=(kc == KC - 1)))
                        if kc < KC - 1:
                            steps.append(lambda fn=fn: S.raw('pe', fn))
                        else:
                            def last(fn=fn, ci=ci, bk=bk):
                                S.group('pe', [fn], reads=[w1B[iw], h2B], writes=[PSB[bk]])
                                sq, sB = sqs[ci % 2], sqB[ci % 2]
                                S.op('act', lambda e: e.activation(out=sq[:, :], in_=PS[bk][:, 0:TF], func=AF.Square),
                                     reads=[PSB[bk]], writes=[sB])
                                S.op('dve', lambda e: e.scalar_tensor_tensor(
                                    out=u_[:, ci, :], in0=PS[bk][:, 0:TF], scalar=0.0, in1=sq[:, :], op0=ALU.is_gt,
                                    op1=ALU.mult), reads=[PSB[bk], sB], writes=[uB_])
                            steps.append(last)
                return steps

            load_w(0)
            for gi, (ft, fg) in enumerate(gl):
                to0 = ft * TF
                iw = gi % 2
                u_, uB_ = ut[iw], utB[iw]
                if fg == 0:
                    for k4 in range(0, KC, 8):
                        k4e = min(KC, k4 + 8)
                        S.dma('sp', acc[:, k4:k4e, :],
                              x1T_d.ap()[k4 * 128:k4e * 128, to0:to0 + TF].rearrange("(k p) t -> p k t", p=128),
                              writes=accB[k4:k4e])
                    norm_mod(ph, "n2", acc, accB, TF, A2, SH2, h2, h2B, 7, sqs, sqB, tmps, tmpB, rs, rstd, rB)
                    for stp in ffn1_pieces(gi):
                        stp()
                if gi + 1 < len(gl):
                    load_w(gi + 1)
                nxt = ffn1_pieces(gi + 1) if (gi + 1 < len(gl) and gl[gi + 1][0] == ft) else []
                per = -(-len(nxt) // KC) if nxt else 0
                for fo in range(KC):
                    bk = 3 + ((gi * KC + fo) % 4)
                    fns = [lambda e, kc=kc, fo=fo, bk=bk, iw=iw, u_=u_: e.matmul(
                        PS[bk][:, 0:TF], lhsT=w2[iw][:, kc, fo * 128:(fo + 1) * 128], rhs=u_[:, kc, :],
                        start=(kc == 0), stop=(kc == NFC - 1)) for kc in range(NFC)]
                    S.group('pe', fns, reads=[w2B[iw], uB_], writes=[PSB[bk]])
                    S.op('dve', lambda e, fo=fo, bk=bk: e.scalar_tensor_tensor(
                        out=acc[:, fo, :], in0=PS[bk][:, 0:TF], scalar=modsb[:, G2 + fo:G2 + fo + 1],
                        in1=acc[:, fo, :], op0=ALU.mult, op1=ALU.add), reads=[PSB[bk], accB[fo]], writes=[accB[fo]])
                    for stp in nxt[fo * per:(fo + 1) * per]:
                        stp()
                if fg < NGF - 1:
                    continue
                norm_stats(acc, accB, TF, 7, sqs, sqB, rs, rstd, rB)
                for kc in range(KC):
                    S.op('dve', lambda e, kc=kc: e.scalar_tensor_tensor(
                        out=acc[:, kc, :], in0=acc[:, kc, :], scalar=nfwt[:, kc:kc + 1], in1=rstd[:, 0:TF],
                        op0=ALU.mult, op1=ALU.mult), reads=[accB[kc], rB, CONST], writes=[accB[kc]])
                cpb = min(4, KC)
                for tb in range(NFB):
                    o_, oB_ = ot[0], otB[0]
                    for g4 in range(KC // cpb):
                        bk = 3 + (nev % 4)
                        fns = [lambda e, i=i, g4=g4, bk=bk, tb=tb: e.transpose(
                            PS[bk][:, i * 128:(i + 1) * 128], acc[:, g4 * cpb + i, tb * 128:(tb + 1) * 128], identf[:, :])
                            for i in range(cpb)]
                        S.group('pe', fns, reads=accB[g4 * cpb:(g4 + 1) * cpb] + [CONST], writes=[PSB[bk]])
                        dsto = o_[:, g4 * cpb * 128:(g4 + 1) * cpb * 128]
                        if nev % 2 == 0:
                            S.op('act', lambda e, dsto=dsto, bk=bk: e.activation(out=dsto, in_=PS[bk][:, 0:cpb * 128], func=AF.Copy),
                                 reads=[PSB[bk]], writes=[oB_])
                        else:
                            S.op('dve', lambda e, dsto=dsto, bk=bk: e.tensor_copy(out=dsto, in_=PS[bk][:, 0:cpb * 128]),
                                 reads=[PSB[bk]], writes=[oB_])
                        nev += 1
                    S.dma('sp', out_d.ap()[to0 + tb * 128:to0 + (tb + 1) * 128, :], o_[:, :], reads=[oB_])
            S.barrier(PSB + [CONST])

        with nc.Block() as block:
            @block.tensor
            def _(eng):
                for f in engs['pe'].ops:
                    f(eng)

            @block.scalar
            def _(eng):
                for f in engs['act'].ops:
                    f(eng)

            @block.vector
            def _(eng):
                for f in engs['dve'].ops:
                    f(eng)

            @block.gpsimd
            def _(eng):
                for f in engs['pool'].ops:
                    f(eng)

            @block.sync
            def _(eng):
                for f in engs['sp'].ops:
                    f(eng)
    return nc


def make_consts(cfg, s):
    c = cfg
    bf = ml_dtypes.bfloat16
    j = np.arange(128)
    k = {}
    k["k_identf"] = np.eye(128, dtype=np.float32)
    k["k_identb"] = np.eye(128, dtype=np.float32).astype(bf)
    k["k_negtri"] = (-(j[:, None] >= j[None, :]).astype(np.float32)).astype(bf)
    k["k_negones"] = (-np.ones((128, 128), np.float32)).astype(bf)
    k["k_onesf"] = np.ones((128, 128), np.float32)
    ndb = c.TQ // 128
    q = np.arange(c.TQ)
    sbm = np.zeros((128, ndb, c.TQ), np.float32)
    for dg in range(ndb):
        sbm[:, dg, :] = (dg * 128 + j[:, None]) < q[None, :]
    k["k_sbmask"] = sbm.astype(bf)
    ntap = len(c.taps)
    dm = np.zeros((128, ntap, 128), np.float32)
    for ti, (g, dl) in enumerate(c.taps):
        w, r = c.groups[g]
        diff = dl * 128 + (j[None, :] - j[:, None])
        dm[:, ti, :] = (diff >= 0) & (diff <= w) & (diff % r == 0)
    k["k_dmask"] = dm.astype(bf)
    k["k_dmaskc"] = (dm * float(s)).astype(bf)
    k["k_cbias"] = np.full((128, 1), 0.0 if s == 1 else -60.0, np.float32)
    half = 64
    invf = (10000.0 ** (-np.arange(half, dtype=np.float32) / half)).astype(np.float32)
    k["k_invf"] = np.broadcast_to(invf[None, :], (128, half)).copy()
    return k


def core_inputs(cfg, b, s, x, cvec, positions, ada_w, ada_b, norm_mix_w, w_in, w_o_sb, w_o_dil, w_out,
                norm_mlp_w, w_ff1, w_ff2, norm_final_w):
    c = cfg
    T, KC = c.T, c.KC
    own = slice(s * T, (s + 1) * T)
    oth = slice((1 - s) * T, (2 - s) * T)
    m = {}
    m["xl"] = np.ascontiguousarray(np.concatenate([x[b, oth], x[b, own]], axis=0))
    m["c_t"] = np.ascontiguousarray(cvec[b].reshape(KC, 128).T)
    pl = np.concatenate([positions[b, oth], positions[b, own]], axis=0).astype(np.int32)
    m["pos_t"] = np.ascontiguousarray(pl.reshape(c.NT, 128).T)
    m["ada_w"] = ada_w
    m["adab_t"] = np.ascontiguousarray(ada_b.reshape(6 * KC, 128).T)
    m["nw1_t"] = np.ascontiguousarray(norm_mix_w.reshape(KC, 128).T)
    m["nw2_t"] = np.ascontiguousarray(norm_mlp_w.reshape(KC, 128).T)
    m["nfw_t"] = np.ascontiguousarray(norm_final_w.reshape(KC, 128).T)
    m["w_in"] = w_in
    m["w_o_sb"] = w_o_sb
    m["w_o_dil"] = w_o_dil
    m["w_out"] = w_out
    m["w_ff1"] = w_ff1
    m["w_ff2"] = w_ff2
    m.update(make_consts(c, s))
    return m


def kernel(x, c, positions, ada_w, ada_b, norm_mix_w, w_in, w_o_sb, w_o_dil, w_out, norm_mlp_w, w_ff1, w_ff2,
           norm_final_w):
    cfg = REAL
    f = lambda a: np.asarray(a)
    x, cvec, positions = f(x), f(c), f(positions)
    args = [f(a)[0] for a in (ada_w, ada_b, norm_mix_w, w_in, w_o_sb, w_o_dil, w_out, norm_mlp_w, w_ff1, w_ff2)]
    ada_w, ada_b, norm_mix_w, w_in, w_o_sb, w_o_dil, w_out, norm_mlp_w, w_ff1, w_ff2 = args
    nfw = f(norm_final_w)
    nc = build(cfg)
    in_maps = []
    for core in range(8):
        b, s = core // 2, core % 2
        in_maps.append(core_inputs(cfg, b, s, x, cvec, positions, ada_w, ada_b, norm_mix_w, w_in, w_o_sb, w_o_dil,
                                   w_out, norm_mlp_w, w_ff1, w_ff2, nfw))
    res = run_bass_kernel_spmd(nc, in_maps, core_ids=list(range(8)))
    out = np.zeros((4, 2 * cfg.T, cfg.D), np.float32)
    for core in range(8):
        b, s = core // 2, core % 2
        out[b, s * cfg.T:(s + 1) * cfg.T] = res.results[core]["out"]
    return out
```

```python
import math
import contextlib
import numpy as np
import ml_dtypes
import concourse.bass as bass
import concourse.mybir as mybir
from concourse.bass_utils import run_bass_kernel_spmd

F32 = mybir.dt.float32
BF16 = mybir.dt.bfloat16
I32 = mybir.dt.int32
AF = mybir.ActivationFunctionType
ALU = mybir.AluOpType
MAGIC = 12582912.0
GATES_IN_P3 = True
EPS = 1e-6


class Cfg:
    def __init__(self, D, T, NH_SB, NH_DIL, groups, DFF):
        self.D, self.T, self.L = D, T, 2 * T
        self.NH_SB, self.NH_DIL, self.groups, self.DFF = NH_SB, NH_DIL, groups, DFF
        self.KC = D // 128
        self.D_SB = NH_SB * 128
        self.D_DIL = NH_DIL * 128
        self.NG = len(groups)
        self.D_IN = 3 * self.D_SB + 3 * self.NG * self.D_DIL + 2 * D
        self.NT = self.L // 128
        self.NTO = T // 128
        self.TS = min(1024, T)
        self.PW = min(512, self.TS)
        self.TQ = min(512, T)
        self.TF = min(512, T)
        self.taps = [(g, d) for g, (w, r) in enumerate(groups) for d in range(w // 128 + 1)]


REAL = Cfg(4096, 2048, 16, 8, ((128, 1), (512, 4), (2048, 16)), 16384)


class Buf:
    def __init__(self, name):
        self.name = name
        self.w = None
        self.r = []


class Eng:
    def __init__(self, name, sem, dsems):
        self.name, self.sem, self.dsems = name, sem, dsems
        self.cnt = 0
        self.dcnt = [0] * len(dsems)
        self.dn = 0
        self.seen = {}
        self.ops = []


class Sched:
    def __init__(self, nc, engs):
        self.nc = nc
        self.E = engs
        self.all_tokens = {}

    def _wait(self, e, deps):
        best = {}
        for (sem, val, src) in deps:
            if src == 'pe' and e.name == 'pe':
                continue
            if id(sem) not in best or best[id(sem)][1] < val:
                best[id(sem)] = (sem, val)
        for (sem, val) in best.values():
            if e.seen.get(id(sem), 0) >= val:
                continue
            e.seen[id(sem)] = val
            e.ops.append(lambda eng, sem=sem, val=val: eng.wait_ge(sem, val))

    def begin(self, en, reads=(), writes=()):
        self._wait(self.E[en], self._deps(reads, writes))

    def raw(self, en, fn):
        self.E[en].ops.append(lambda eng, fn=fn: fn(eng))

    def _deps(self, reads, writes):
        deps = []
        for b in reads:
            if b.w is not None:
                deps.append(b.w)
        for b in writes:
            if b.w is not None:
                deps.append(b.w)
            deps.extend(b.r)
        return deps

    def _mark(self, tok, reads, writes):
        for b in reads:
            b.r.append(tok)
        for b in writes:
            b.w = tok
            b.r = []

    def group(self, en, fns, reads=(), writes=()):
        e = self.E[en]
        self._wait(e, self._deps(reads, writes))
        e.cnt += 1
        n = len(fns)
        for i, fn in enumerate(fns):
            if i == n - 1:
                e.ops.append(lambda eng, fn=fn, sem=e.sem: fn(eng).then_inc(sem, 1))
            else:
                e.ops.append(lambda eng, fn=fn: fn(eng))
        tok = (e.sem, e.cnt, en)
        self._mark(tok, reads, writes)
        return tok

    def op(self, en, fn, reads=(), writes=()):
        return self.group(en, [fn], reads, writes)

    def dma(self, en, out, in_, reads=(), writes=(), **kw):
        e = self.E[en]
        j = e.dn % len(e.dsems)
        e.dn += 1
        sem = e.dsems[j]
        deps = self._deps(reads, writes)
        if e.dcnt[j] > 0:
            deps.append((sem, e.dcnt[j], 'dma'))
        self._wait(e, deps)
        e.dcnt[j] += 16
        e.ops.append(lambda eng, out=out, in_=in_, sem=sem, kw=kw:
                     eng.dma_start(out=out, in_=in_, **kw).then_inc(sem, 16))
        tok = (sem, e.dcnt[j], 'dma')
        self._mark(tok, reads, writes)
        return tok

    def barrier(self, bufs=()):
        toks = []
        for e in self.E.values():
            if e.cnt > 0:
                toks.append((e.sem, e.cnt, 'bar'))
            for j, s in enumerate(e.dsems):
                if e.dcnt[j] > 0:
                    toks.append((s, e.dcnt[j], 'bar'))
        for e in self.E.values():
            self._wait(e, toks)
        for b in bufs:
            b.w = None
            b.r = []


def build(cfg, debug_outs=()):
    c = cfg
    D, T, L, KC = c.D, c.T, c.L, c.KC
    nc = bass.Bass("TRN2", target_bir_lowering=False)

    def din(name, shape, dt):
        return nc.dram_tensor(name, list(shape), dt, kind="ExternalInput")

    def dscr(name, shape, dt):
        kind = "ExternalOutput" if name in debug_outs else "Internal"
        return nc.dram_tensor(name, list(shape), dt, kind=kind)

    NTAP = len(c.taps)
    xl = din("xl", [L, D], F32)
    c_t = din("c_t", [128, KC], F32)
    pos_t = din("pos_t", [128, c.NT], I32)
    ada_w = din("ada_w", [D, 6 * D], F32)
    adab_t = din("adab_t", [128, 6 * KC], F32)
    nw1_t = din("nw1_t", [128, KC], F32)
    nw2_t = din("nw2_t", [128, KC], F32)
    nfw_t = din("nfw_t", [128, KC], F32)
    w_in = din("w_in", [D, c.D_IN], F32)
    w_o_sb = din("w_o_sb", [c.D_SB, D], F32)
    w_o_dil = din("w_o_dil", [c.D_DIL, D], F32)
    w_out = din("w_out", [D, D], F32)
    w_ff1 = din("w_ff1", [D, c.DFF], F32)
    w_ff2 = din("w_ff2", [c.DFF, D], F32)
    k_identf = din("k_identf", [128, 128], F32)
    k_identb = din("k_identb", [128, 128], BF16)
    k_negtri = din("k_negtri", [128, 128], BF16)
    k_negones = din("k_negones", [128, 128], BF16)
    k_onesf = din("k_onesf", [128, 128], F32)
    k_sbmask = din("k_sbmask", [128, c.TQ // 128, c.TQ], BF16)
    k_dmask = din("k_dmask", [128, NTAP, 128], BF16)
    k_dmaskc = din("k_dmaskc", [128, NTAP, 128], BF16)
    k_cbias = din("k_cbias", [128, 1], F32)
    k_invf = din("k_invf", [128, 64], F32)
    out_d = nc.dram_tensor("out", [T, D], F32, kind="ExternalOutput")
    hT_d = dscr("hT_d", [D, L], BF16)
    xT_d = dscr("xT_d", [D, T], F32)
    qT_d = dscr("qT_d", [c.D_SB, T], BF16)
    kT_d = dscr("kT_d", [c.D_SB, L], BF16)
    v_d = dscr("v_d", [L, c.D_SB], BF16)
    qdT_d = dscr("qdT_d", [c.NG * c.D_DIL, T], BF16)
    kdT_d = dscr("kdT_d", [c.NG * c.D_DIL, L], BF16)
    vd_d = dscr("vd_d", [L, c.NG * c.D_DIL], BF16)
    gT_d = dscr("gT_d", [2 * D, T], F32)
    yaT_d = dscr("yaT_d", [c.D_SB, T], BF16)
    ybT_d = dscr("ybT_d", [c.D_DIL, T], BF16)
    x1T_d = dscr("x1T_d", [D, T], F32)
    h2T_d = dscr("h2T_d", [D, T], BF16)
    mod_d = dscr("mod_d", [128, 6 * KC], F32)
    SWF = 256 if c.DFF >= 256 else 128
    NGF = c.DFF // SWF
    w1b_d = dscr("w1b_d", [NGF, 128, KC, SWF], BF16)
    w2b_d = dscr("w2b_d", [c.DFF, D], BF16)

    es = contextlib.ExitStack()
    with es:
        def sem(name):
            return es.enter_context(nc.semaphore(name))

        engs = {
            'pe': Eng('pe', sem('s_pe'), []),
            'act': Eng('act', sem('s_act'), []),
            'dve': Eng('dve', sem('s_dve'), []),
            'pool': Eng('pool', sem('s_pool'), [sem(f'd_pool{i}') for i in range(6)]),
            'sp': Eng('sp', sem('s_sp'), [sem(f'd_sp{i}') for i in range(12)]),
        }
        S = Sched(nc, engs)

        def sb(name, shape, dt, stack=es):
            return stack.enter_context(nc.sbuf_tensor(name, list(shape), dt))

        PS = [es.enter_context(nc.psum_tensor(f"ps{i}", [128, 512], F32)) for i in range(8)]
        PSB = [Buf(f"ps{i}") for i in range(8)]

        identf = sb("identf", [128, 128], F32)
        identb = sb("identb", [128, 128], BF16)
        negtri = sb("negtri", [128, 128], BF16)
        negones = sb("negones", [128, 128], BF16)
        onesf = sb("onesf", [128, 128], F32)
        cbias = sb("cbias", [128, 1], F32)
        invf = sb("invf", [128, 64], F32)
        c_eps = sb("c_eps", [128, 1], F32)
        c_one = sb("c_one", [128, 1], F32)
        c_zero = sb("c_zero", [128, 1], F32)
        c_pi = sb("c_pi", [128, 1], F32)
        modsb = sb("modsb", [128, 6 * KC], F32)
        A1 = sb("A1", [128, KC], F32)
        A2 = sb("A2", [128, KC], F32)
        nfwt = sb("nfwt", [128, KC], F32)
        sc_bf = sb("sc_bf", [128, KC], BF16)
        adab = sb("adab", [128, 6 * KC], F32)
        nw2 = sb("nw2", [128, KC], F32)
        tmpk = sb("tmpk", [128, KC], F32)
        MB = {n: Buf(n) for n in ["sc_bf", "adab", "nw", "tmpk", "mod1", "mod2"]}
        SW0 = 256
        nsl0 = 6 * D // SW0
        nfirst = 2 * D // SW0
        nd0 = [0]
        CONST = Buf("const")

        def mod_dma(upto, wsl, wslB):
            while nd0[0] <= min(upto, nsl0 - 1):
                i = nd0[0]
                dst, src = wload(wsl[i % 3][:, :, :], ada_w.ap(), 0, D, i * SW0, SW0)
                S.dma('pool', dst, src, writes=[wslB[i % 3]])
                nd0[0] += 1

        def mod_slab(sl, wsl, wslB, bank, last):
            wb, wB = wsl[sl % 3], wslB[sl % 3]
            mod_dma(min(sl + 2, last), wsl, wslB)
            for jj in range(SW0 // 128):
                col = sl * (SW0 // 128) + jj
                fns = [lambda e, kc=kc, jj=jj, col=col, wb=wb: e.matmul(
                    PS[bank][:, col:col + 1], lhsT=wb[:, kc, jj * 128:(jj + 1) * 128],
                    rhs=sc_bf[:, kc:kc + 1], start=(kc == 0), stop=(kc == KC - 1)) for kc in range(KC)]
                S.group('pe', fns, reads=[wB, MB["sc_bf"]], writes=[PSB[bank]])

        def mod_finish(c0, c1, Mb, Ax, nwx, SCx, bank):
            S.op('dve', lambda e: e.tensor_tensor(out=modsb[:, c0:c1], in0=PS[bank][:, c0:c1], in1=adab[:, c0:c1], op=ALU.add),
                 reads=[PSB[bank], MB["adab"]], writes=[Mb])
            S.op('dve', lambda e: e.tensor_scalar(out=tmpk[:, :], in0=modsb[:, SCx:SCx + KC], scalar1=1.0,
                                                  scalar2=None, op0=ALU.add), reads=[Mb], writes=[MB["tmpk"]])
            S.op('dve', lambda e: e.tensor_tensor(out=Ax[:, :], in0=tmpk[:, :], in1=nwx[:, :], op=ALU.mult),
                 reads=[MB["tmpk"], MB["nw"]], writes=[Mb])

        def ld_const(dst, src):
            S.dma('sp', dst, src, writes=[CONST])

        ld_const(identf[:, :], k_identf.ap())
        ld_const(identb[:, :], k_identb.ap())
        ld_const(negtri[:, :], k_negtri.ap())
        ld_const(negones[:, :], k_negones.ap())
        ld_const(onesf[:, :], k_onesf.ap())
        ld_const(cbias[:, :], k_cbias.ap())
        ld_const(invf[:, :], k_invf.ap())
        ld_const(nfwt[:, :], nfw_t.ap())
        S.op('dve', lambda e: e.memset(c_eps[:, :], EPS), writes=[CONST])
        S.op('dve', lambda e: e.memset(c_one[:, :], 1.0), writes=[CONST])
        S.op('dve', lambda e: e.memset(c_zero[:, :], 0.0), writes=[CONST])
        S.op('dve', lambda e: e.memset(c_pi[:, :], math.pi), writes=[CONST])

        SH1, SC1, G1, SH2, SC2, G2 = [i * KC for i in range(6)]

        def wload(dst, w_ap, r0, rows, c0, cols):
            return (dst, w_ap[r0:r0 + rows, c0:c0 + cols].rearrange("(k p) f -> p k f", p=128))

        conv_steps = []
        for fg_ in range(NGF):
            def cv1(fg_=fg_):
                src = w_ff1.ap()[:, fg_ * SWF:(fg_ + 1) * SWF].rearrange("(k p) f -> p k f", p=128)
                S.dma('pool', w1b_d.ap()[fg_], src)
            conv_steps.append(cv1)

            def cv2(fg_=fg_):
                npc = max(1, D // 2048)
                src = w_ff2.ap()[fg_ * SWF:(fg_ + 1) * SWF, :].rearrange("r (a f) -> r a f", a=npc)
                dst = w2b_d.ap()[fg_ * SWF:(fg_ + 1) * SWF, :].rearrange("r (a f) -> r a f", a=npc)
                S.dma('pool', dst, src)
            conv_steps.append(cv2)
        cpos = [0]

        def conv_run(n):
            for f_ in conv_steps[cpos[0]:cpos[0] + n]:
                f_()
            cpos[0] = min(len(conv_steps), cpos[0] + n)

        def norm_stats(xTt, xB, W, bank, sqs, sqB, rs, rstd, rB):
            for kc in range(KC):
                sq, sB = sqs[kc % 2], sqB[kc % 2]
                S.op('act', lambda e, kc=kc, sq=sq: e.activation(out=sq[:, 0:W], in_=xTt[:, kc, 0:W], func=AF.Square),
                     reads=[xB[kc * len(xB) // KC]], writes=[sB])
                S.op('pe', lambda e, kc=kc, sq=sq: e.matmul(PS[bank][:, 0:W], lhsT=onesf[:, :], rhs=sq[:, 0:W],
                                                            start=(kc == 0), stop=(kc == KC - 1)),
                     reads=[sB, CONST], writes=[PSB[bank]])
            S.op('act', lambda e: e.activation(out=rs[:, 0:W], in_=PS[bank][:, 0:W], func=AF.Sqrt, bias=c_eps[:, 0:1],
                                               scale=1.0 / D), reads=[PSB[bank], CONST], writes=[rB])
            S.op('dve', lambda e: e.reciprocal(out=rstd[:, 0:W], in_=rs[:, 0:W]), reads=[rB], writes=[rB])

        def norm_mod(ph, tag, xTt, xB, W, A, Bcol0, hst, hB, bank, sqs, sqB, tmps, tmpB, rs, rstd, rB, xr=()):
            norm_stats(xTt, xB, W, bank, sqs, sqB, rs, rstd, rB)
            for kc in range(KC):
                tm, tB = tmps[kc % 2], tmpB[kc % 2]
                S.op('dve', lambda e, kc=kc, tm=tm: e.scalar_tensor_tensor(
                    out=tm[:, 0:W], in0=xTt[:, kc, 0:W], scalar=A[:, kc:kc + 1], in1=rstd[:, 0:W],
                    op0=ALU.mult, op1=ALU.mult), reads=[xB[kc * len(xB) // KC], rB] + list(xr), writes=[tB])
                S.op('act', lambda e, kc=kc, tm=tm: e.activation(
                    out=hst[:, kc, 0:W], in_=tm[:, 0:W], func=AF.Identity,
                    bias=modsb[:, Bcol0 + kc:Bcol0 + kc + 1], scale=1.0), reads=[tB] + list(xr), writes=[hB])

        W1 = min(512, T)
        nb1 = W1 // 128
        with contextlib.ExitStack() as ph:
            c_sb = sb("c_sb", [128, KC], F32, ph)
            nw1 = sb("nw1", [128, KC], F32, ph)
            wsl = [sb(f"w0_{i}", [128, KC, SW0], BF16, ph) for i in range(3)]
            wslB = [Buf(f"w0_{i}") for i in range(3)]
            B = MB
            B["c_sb"] = Buf("c_sb")
            xts = [sb(f"xt{i}", [128, D], F32, ph) for i in range(2)]
            xtB = [Buf(f"xt{i}") for i in range(2)]
            xTt = sb("xTt", [128, KC, W1], F32, ph)
            xTBs = [Buf(f"xTt{i}") for i in range(max(1, KC // min(4, KC)))]
            hst = sb("hst", [128, KC, W1], BF16, ph)
            hB = Buf("hst")
            sqs = [sb(f"sq{i}", [128, W1], F32, ph) for i in range(2)]
            sqB = [Buf(f"sq{i}") for i in range(2)]
            tmps = [sb(f"tm{i}", [128, W1], F32, ph) for i in range(2)]
            tmpB = [Buf(f"tm{i}") for i in range(2)]
            rs = sb("rs", [128, W1], F32, ph)
            rstd = sb("rstd", [128, W1], F32, ph)
            rB = Buf("rs")

            S.dma('sp', c_sb[:, :], c_t.ap(), writes=[B["c_sb"]])
            S.dma('sp', adab[:, :], adab_t.ap(), writes=[B["adab"]])
            S.dma('sp', nw1[:, :], nw1_t.ap(), writes=[B["nw"]])
            S.dma('sp', nw2[:, :], nw2_t.ap(), writes=[B["nw"]])
            S.op('act', lambda e: e.activation(out=sc_bf[:, :], in_=c_sb[:, :], func=AF.Silu),
                 reads=[B["c_sb"]], writes=[B["sc_bf"]])
            for sl in range(nfirst):
                mod_slab(sl, wsl, wslB, 0, nfirst - 1)
            mod_finish(0, 2 * KC, B["mod1"], A1, nw1, SC1, 0)
            ntl = L // W1
            cpb = min(4, KC)
            ev = 0
            for tt in range(ntl):
                for j in range(nb1):
                    tb = tt * nb1 + j
                    xt, xB_ = xts[tb % 2], xtB[tb % 2]
                    S.dma('sp', xt[:, :], xl.ap()[tb * 128:(tb + 1) * 128, :], writes=[xB_])
                    for g4 in range(KC // cpb):
                        bk = 1 + (ev % 4)
                        fns = [lambda e, i=i, g4=g4, xt=xt, bk=bk: e.transpose(
                            PS[bk][:, i * 128:(i + 1) * 128], xt[:, (g4 * cpb + i) * 128:(g4 * cpb + i + 1) * 128],
                            identf[:, :]) for i in range(cpb)]
                        S.group('pe', fns, reads=[xB_, CONST], writes=[PSB[bk]])
                        src = PS[bk][:, 0:cpb * 128].rearrange("p (a b) -> p a b", a=cpb)
                        dst = xTt[:, g4 * cpb:(g4 + 1) * cpb, j * 128:(j + 1) * 128]
                        if ev % 2 == 0:
                            S.op('act', lambda e, dst=dst, src=src: e.activation(out=dst, in_=src, func=AF.Copy),
                                 reads=[PSB[bk]], writes=[xTBs[g4]])
                        else:
                            S.op('dve', lambda e, dst=dst, src=src: e.tensor_copy(out=dst, in_=src),
                                 reads=[PSB[bk]], writes=[xTBs[g4]])
                        ev += 1
                norm_mod(ph, "n1", xTt, xTBs, W1, A1, SH1, hst, hB, 5, sqs, sqB, tmps, tmpB, rs, rstd, rB, xr=[B["mod1"]])
                S.dma('sp', hT_d.ap()[:, tt * W1:(tt + 1) * W1].rearrange("(k p) t -> p k t", p=128), hst[:, :, :],
                      reads=[hB])
                if tt * W1 >= T:
                    o0 = tt * W1 - T
                    S.dma('sp', xT_d.ap()[:, o0:o0 + W1].rearrange("(k p) t -> p k t", p=128), xTt[:, :, :],
                          reads=xTBs)
            S.barrier(PSB + [CONST])

        TS, PW = c.TS, c.PW
        NHF = TS // PW
        NTB = TS // 128
        o_qa = 0
        o_ka = c.D_SB
        o_va = 2 * c.D_SB
        o_qb = 3 * c.D_SB
        DG = c.NG * c.D_DIL
        o_kb = o_qb + DG
        o_vb = o_kb + DG
        o_ga = o_vb + DG
        o_gb = o_ga + D
        segs = [("qa", o_qa, c.D_SB, "fm", False), ("ka", o_ka, c.D_SB, "fm", True), ("va", o_va, c.D_SB, "tm", True),
                ("qb", o_qb, DG, "rope", False), ("kb", o_kb, DG, "rope", True), ("vb", o_vb, DG, "tm", True),
                ("ga", o_ga, D, "fm", False), ("gb", o_gb, D, "fm", False)]
        with contextlib.ExitStack() as ph:
            cosT = sb("cosT", [128, c.NT, 64], F32, ph)
            sinT = sb("sinT", [128, c.NT, 64], F32, ph)
            cosQ = sb("cosQ", [128, c.NTO, 64], F32, ph)
            sinQ = sb("sinQ", [128, c.NTO, 64], F32, ph)
            with contextlib.ExitStack() as ph0:
                phx = ph
                ph = ph0
                posi = sb("posi", [128, c.NT], I32, ph)
                posf = sb("posf", [128, c.NT], F32, ph)
                ang = sb("ang", [128, c.NT, 64], F32, ph)
                t1 = sb("rt1", [128, c.NT, 64], F32, ph)
                t2 = sb("rt2", [128, c.NT, 64], F32, ph)
                B = {n: Buf(n) for n in ["posi", "posf", "ang", "t1", "t2"]}
                S.dma('sp', posi[:, :], pos_t.ap(), writes=[B["posi"]])
                S.op('dve', lambda e: e.tensor_copy(out=posf[:, :], in_=posi[:, :]), reads=[B["posi"]], writes=[B["posf"]])
                for t in range(c.NT):
                    S.op('dve', lambda e, t=t: e.tensor_scalar(out=ang[:, t, :], in0=invf[:, :], scalar1=posf[:, t:t + 1],
                                                               scalar2=None, op0=ALU.mult),
                         reads=[B["posf"], CONST], writes=[B["ang"]])
                C1 = 6.28125
                C2 = 2.0 * math.pi - C1
                INV2PI = 1.0 / (2.0 * math.pi)

                def sin_table(dst, shift):
                    if shift != 0.0:
                        S.op('dve', lambda e: e.tensor_scalar(out=t2[:, :, :], in0=ang[:, :, :], scalar1=shift, scalar2=None,
                                                              op0=ALU.add), reads=[B["ang"]], writes=[B["t2"]])
                        th, thB = t2, B["t2"]
                    else:
                        th, thB = ang, B["ang"]
                    S.op('dve', lambda e: e.tensor_scalar(out=t1[:, :, :], in0=th[:, :, :], scalar1=INV2PI, scalar2=MAGIC,
                                                          op0=ALU.mult, op1=ALU.add), reads=[thB], writes=[B["t1"]])
                    S.op('dve', lambda e: e.tensor_scalar(out=t1[:, :, :], in0=t1[:, :, :], scalar1=-MAGIC, scalar2=None,
                                                          op0=ALU.add), reads=[B["t1"]], writes=[B["t1"]])
                    S.op('dve', lambda e: e.scalar_tensor_tensor(out=th[:, :, :] if th is t2 else t2[:, :, :], in0=t1[:, :, :],
                                                                 scalar=-C1, in1=th[:, :, :], op0=ALU.mult, op1=ALU.add),
                         reads=[B["t1"], thB], writes=[B["t2"]])
                    S.op('dve', lambda e: e.scalar_tensor_tensor(out=t2[:, :, :], in0=t1[:, :, :], scalar=-C2, in1=t2[:, :, :],
                                                                 op0=ALU.mult, op1=ALU.add),
                         reads=[B["t1"], B["t2"]], writes=[B["t2"]])
                    S.op('dve', lambda e: e.tensor_scalar(out=t2[:, :, :], in0=t2[:, :, :], scalar1=3.14159, scalar2=-3.14159,
                                                          op0=ALU.min, op1=ALU.max), reads=[B["t2"]], writes=[B["t2"]])
                    S.op('act', lambda e: e.activation(out=dst[:, :, :], in_=t2[:, :, :], func=AF.Sin),
                         reads=[B["t2"]], writes=[CONST])

                sin_table(sinT, 0.0)
                sin_table(cosT, math.pi / 2.0)
                qs = 128.0 ** -0.5
                S.op('dve', lambda e: e.tensor_scalar(out=cosQ[:, :, :], in0=cosT[:, c.NTO:, :], scalar1=qs, scalar2=None,
                                                      op0=ALU.mult), reads=[CONST], writes=[CONST])
                S.op('dve', lambda e: e.tensor_scalar(out=sinQ[:, :, :], in0=sinT[:, c.NTO:, :], scalar1=qs, scalar2=None,
                                                      op0=ALU.mult), reads=[CONST], writes=[CONST])
                S.barrier(PSB + [CONST])
                ph = phx
            hTt = sb("hTt", [128, KC, TS], BF16, ph)
            hTB = Buf("hTt")
            SW = 256
            wsl = [sb(f"w2_{i}", [128, KC, SW], BF16, ph) for i in range(3)]
            wslB = [Buf(f"w2_{i}") for i in range(3)]
            stb = [sb(f"stb{i}", [128, 4, TS], BF16, ph) for i in range(2)]
            stbB = [Buf(f"stb{i}") for i in range(2)]
            stf = sb("stf", [128, 4, TS], F32, ph)
            stfB = Buf("stf")
            stt = [sb(f"stt{i}", [128, NTB, SW], BF16, ph) for i in range(2)]
            sttB = [Buf(f"stt{i}") for i in range(2)]
            rp = [sb(f"rp{i}", [128, SW], BF16, ph) for i in range(2)]
            rpB = [Buf(f"rp{i}") for i in range(2)]
            ra = [sb(f"ra{i}", [128, SW // 2], F32, ph) for i in range(2)]
            raB = [Buf(f"ra{i}") for i in range(2)]
            nsl = 0
            nbk = 0
            nst = 0
            nrp = 0
            NW = 3
            slist = []
            for st in range(L // TS):
                own = st * TS >= T
                t0 = st * TS
                for (nm, o0, wd, kind, allt) in segs:
                    if not own and not allt:
                        continue
                    if GATES_IN_P3 and nm in ("ga", "gb"):
                        continue
                    for s0 in range(0, wd, min(SW, wd)):
                        sw = min(SW, wd - s0)
                        tb_lo = 0
                        if (not own) and nm in ("kb", "vb"):
                            g_lo, g_hi = s0 // c.D_DIL, (s0 + sw - 1) // c.D_DIL
                            wmax = max(c.groups[g][0] for g in range(g_lo, g_hi + 1))
                            tb_lo = max(0, (T - wmax - t0) // 128)
                            if tb_lo >= NTB:
                                continue
                        slist.append((st, nm, o0, wd, kind, s0, sw, tb_lo))
            ndma = [0]

            def ensure_dma(upto):
                while ndma[0] <= min(upto, len(slist) - 1):
                    i = ndma[0]
                    (st_, nm_, o0_, wd_, kind_, s0_, sw_, tbl_) = slist[i]
                    dst, src = wload(wsl[i % NW][:, :, 0:sw_], w_in.ap(), 0, D, o0_ + s0_, sw_)
                    S.dma('pool', dst, src, writes=[wslB[i % NW]])
                    ndma[0] += 1

            cur_st = -1
            for si, sl_ in enumerate(slist):
                for _o1 in (0,):
                    for _o2 in (0,):
                        (st, nm, o0, wd, kind, s0, sw, tb_lo) = sl_
                        own = st * TS >= T
                        t0 = st * TS
                        to0 = t0 - T
                        ensure_dma(si + 2)
                        if si % 8 == 7:
                            conv_run(1)
                        if st != cur_st:
                            cur_st = st
                            for k4 in range(0, KC, 8):
                                k4e = min(KC, k4 + 8)
                                S.dma('sp', hTt[:, k4:k4e, :],
                                      hT_d.ap()[k4 * 128:k4e * 128, t0:t0 + TS].rearrange("(k p) t -> p k t", p=128),
                                      writes=[hTB])
                        wb, wB = wsl[si % NW], wslB[si % NW]
                        nch = sw // 128
                        if kind == "fm":
                            gate = nm in ("ga", "gb")
                            if gate:
                                stg, sgB = stf, stfB
                            else:
                                stg, sgB = stb[nst % 2], stbB[nst % 2]
                                nst += 1
                            for ci in range(nch):
                                for hf in range(NHF):
                                    bk = nbk % 6
                                    nbk += 1
                                    fns = [lambda e, kc=kc, ci=ci, hf=hf, wb=wb, bk=bk: e.matmul(
                                        PS[bk][:, 0:PW], lhsT=wb[:, kc, ci * 128:(ci + 1) * 128],
                                        rhs=hTt[:, kc, hf * PW:(hf + 1) * PW], start=(kc == 0), stop=(kc == KC - 1))
                                        for kc in range(KC)]
                                    S.group('pe', fns, reads=[wB, hTB], writes=[PSB[bk]])
                                    dsta = stg[:, ci, hf * PW:(hf + 1) * PW]
                                    if gate:
                                        S.op('act', lambda e, dsta=dsta, bk=bk: e.activation(
                                            out=dsta, in_=PS[bk][:, 0:PW], func=AF.Sigmoid), reads=[PSB[bk]], writes=[sgB])
                                    elif nm == "qa":
                                        S.op('act', lambda e, dsta=dsta, bk=bk: e.activation(
                                            out=dsta, in_=PS[bk][:, 0:PW], func=AF.Copy, scale=qs), reads=[PSB[bk]],
                                            writes=[sgB])
                                    else:
                                        S.op('dve', lambda e, dsta=dsta, bk=bk: e.tensor_copy(out=dsta, in_=PS[bk][:, 0:PW]),
                                             reads=[PSB[bk]], writes=[sgB])
                            if nm == "qa":
                                dd = qT_d.ap()[s0:s0 + sw, to0:to0 + TS]
                            elif nm == "ka":
                                dd = kT_d.ap()[s0:s0 + sw, t0:t0 + TS]
                            elif nm == "ga":
                                dd = gT_d.ap()[s0:s0 + sw, to0:to0 + TS]
                            else:
                                dd = gT_d.ap()[D + s0:D + s0 + sw, to0:to0 + TS]
                            S.dma('sp', dd.rearrange("(k p) t -> p k t", p=128), stg[:, 0:nch, :], reads=[sgB])
                        else:
                            rope = kind == "rope"
                            if rope:
                                stg, sgB = stb[nst % 2], stbB[nst % 2]
                                nst += 1
                            else:
                                stg, sgB = stt[nst % 2], sttB[nst % 2]
                                nst += 1
                            pending = []
                            for tb in range(tb_lo, NTB):
                                bk = nbk % 6
                                nbk += 1
                                fns = [lambda e, kc=kc, tb=tb, wb=wb, bk=bk, sw=sw: e.matmul(
                                    PS[bk][:, 0:sw], lhsT=hTt[:, kc, tb * 128:(tb + 1) * 128], rhs=wb[:, kc, 0:sw],
                                    start=(kc == 0), stop=(kc == KC - 1)) for kc in range(KC)]
                                S.group('pe', fns, reads=[wB, hTB], writes=[PSB[bk]])
                                if not rope:
                                    dsta = stg[:, tb, 0:sw]
                                    if tb % 2 == 0:
                                        S.op('act', lambda e, dsta=dsta, bk=bk, sw=sw: e.activation(
                                            out=dsta, in_=PS[bk][:, 0:sw], func=AF.Copy), reads=[PSB[bk]], writes=[sgB])
                                    else:
                                        S.op('dve', lambda e, dsta=dsta, bk=bk, sw=sw: e.tensor_copy(
                                            out=dsta, in_=PS[bk][:, 0:sw]), reads=[PSB[bk]], writes=[sgB])
                                    continue
                                gtb = (t0 // 128) + tb
                                if nm == "qb":
                                    ct, stn = cosQ[:, gtb - c.NTO, :], sinQ[:, gtb - c.NTO, :]
                                else:
                                    ct, stn = cosT[:, gtb, :], sinT[:, gtb, :]
                                r_, rB_ = rp[nrp % 2], rpB[nrp % 2]
                                a0, a0B = ra[0], raB[0]
                                a1, a1B = ra[1], raB[1]
                                nrp += 1
                                xv = PS[bk][:, 0:sw].rearrange("p (h two d) -> p h two d", two=2, d=64)
                                rv = r_[:, 0:sw].rearrange("p (h two d) -> p h two d", two=2, d=64)
                                a0v = a0[:, 0:nch * 64].rearrange("p (h d) -> p h d", d=64)
                                a1v = a1[:, 0:nch * 64].rearrange("p (h d) -> p h d", d=64)
                                for hh in range(nch):
                                    pass
                                cb = bass.AP(ct.tensor, ct.offset, [list(ct.ap[0]), [0, nch], list(ct.ap[-1])])
                                sbp = bass.AP(stn.tensor, stn.offset, [list(stn.ap[0]), [0, nch], list(stn.ap[-1])])
                                x1, x2 = xv[:, :, 0, :], xv[:, :, 1, :]
                                S.op('dve', lambda e, a0v=a0v, x1=x1, cb=cb: e.tensor_tensor(out=a0v, in0=x1, in1=cb, op=ALU.mult),
                                     reads=[PSB[bk], CONST], writes=[a0B])
                                S.op('dve', lambda e, a1v=a1v, x2=x2, sbp=sbp: e.tensor_tensor(out=a1v, in0=x2, in1=sbp, op=ALU.mult),
                                     reads=[PSB[bk], CONST], writes=[a1B])
                                S.op('dve', lambda e, rv=rv, a0v=a0v, a1v=a1v: e.tensor_tensor(
                                    out=rv[:, :, 0, :], in0=a0v, in1=a1v, op=ALU.subtract), reads=[a0B, a1B], writes=[rB_])
                                S.op('dve', lambda e, a0v=a0v, x2=x2, cb=cb: e.tensor_tensor(out=a0v, in0=x2, in1=cb, op=ALU.mult),
                                     reads=[PSB[bk], CONST], writes=[a0B])
                                S.op('dve', lambda e, a1v=a1v, x1=x1, sbp=sbp: e.tensor_tensor(out=a1v, in0=x1, in1=sbp, op=ALU.mult),
                                     reads=[PSB[bk], CONST], writes=[a1B])
                                S.op('dve', lambda e, rv=rv, a0v=a0v, a1v=a1v: e.tensor_tensor(
                                    out=rv[:, :, 1, :], in0=a0v, in1=a1v, op=ALU.add), reads=[a0B, a1B], writes=[rB_])
                                for f_ in pending:
                                    f_()
                                bk2 = 6 + (nrp % 2)

                                def do_tr(r_=r_, rB_=rB_, bk2=bk2, tb=tb, stg=stg, sgB=sgB, nch=nch):
                                    psb = PS[bk2].bitcast(BF16)
                                    fns = [lambda e, hh=hh: e.transpose(
                                        psb[:, hh * 128:(hh + 1) * 128], r_[:, hh * 128:(hh + 1) * 128], identb[:, :])
                                        for hh in range(nch)]
                                    S.group('pe', fns, reads=[rB_, CONST], writes=[PSB[bk2]])
                                    S.op('act', lambda e: e.activation(
                                        out=stg[:, 0:nch, tb * 128:(tb + 1) * 128],
                                        in_=psb[:, 0:nch * 128].rearrange("p (a b) -> p a b", a=nch), func=AF.Copy),
                                        reads=[PSB[bk2]], writes=[sgB])
                                pending = [do_tr]
                            for f_ in pending:
                                f_()
                            c_lo = tb_lo * 128
                            if rope:
                                if nm == "qb":
                                    dd = qdT_d.ap()[s0:s0 + sw, to0 + c_lo:to0 + TS]
                                else:
                                    dd = kdT_d.ap()[s0:s0 + sw, t0 + c_lo:t0 + TS]
                                S.dma('sp', dd.rearrange("(k p) t -> p k t", p=128), stg[:, 0:nch, c_lo:], reads=[sgB])
                            else:
                                dd = (v_d if nm == "va" else vd_d).ap()[t0 + c_lo:t0 + TS, s0:s0 + sw]
                                S.dma('sp', dd.rearrange("(t p) f -> p t f", p=128), stg[:, tb_lo:, 0:sw], reads=[sgB])
            S.barrier(PSB + [CONST])

        TQ = c.TQ
        NQT = T // TQ
        NDB = TQ // 128
        with contextlib.ExitStack() as ph:
            kTh = [sb(f"kTh{i}", [128, L], BF16, ph) for i in range(2)]
            kTB = [Buf(f"kTh{i}") for i in range(2)]
            vh = [sb(f"vh{i}", [128, c.NT, 128], BF16, ph) for i in range(2)]
            vB = [Buf(f"vh{i}") for i in range(2)]
            qTh = [sb(f"qTh{i}", [128, T], BF16, ph) for i in range(2)]
            qB = [Buf(f"qTh{i}") for i in range(2)]
            sbm = sb("sbm", [128, NDB, TQ], BF16, ph)
            NE, NL, NA = 3, 4, 3
            e_ = [sb(f"e{i}", [128, TQ], F32, ph) for i in range(NE)]
            eB = [Buf(f"e{i}") for i in range(NE)]
            lp = [sb(f"lp{i}", [128, TQ], BF16, ph) for i in range(NL)]
            lpB = [Buf(f"lp{i}") for i in range(NL)]
            lacc = [sb(f"lacc{i}", [128, TQ], BF16, ph) for i in range(3)]
            laccB = [Buf(f"lacc{i}") for i in range(3)]
            am = [sb(f"am{i}", [128, TQ], BF16, ph) for i in range(NA)]
            amB = [Buf(f"am{i}") for i in range(NA)]
            yst = [sb(f"yst{i}", [128, TQ], BF16, ph) for i in range(2)]
            ystB = [Buf(f"yst{i}") for i in range(2)]
            S.dma('sp', sbm[:, :, :], k_sbmask.ap(), writes=[CONST])

            gsteps = []
            if GATES_IN_P3:
                GTS, GPW = c.TS, c.PW
                GNH = GTS // GPW
                GSW = min(256, D)
                hTg = sb("hTg", [128, KC, GTS], BF16, ph)
                hTgB = Buf("hTg")
                wg = [sb(f"wg{i}", [128, KC, GSW], BF16, ph) for i in range(3)]
                wgB = [Buf(f"wg{i}") for i in range(3)]
                gst = [sb(f"gst{i}", [128, GSW // 128, GTS], F32, ph) for i in range(2)]
                gstB = [Buf(f"gst{i}") for i in range(2)]
                gex = [sb(f"gex{i}", [128, GPW], F32, ph) for i in range(2)]
                gexB = [Buf(f"gex{i}") for i in range(2)]
                gslabs = [(st_, sg, s0_) for st_ in range(T // GTS) for sg in range(2) for s0_ in range(0, D, GSW)]
                gd = [0]

                def gate_dma(upto):
                    while gd[0] <= min(upto, len(gslabs) - 1):
                        i = gd[0]
                        st_, sg, s0_ = gslabs[i]
                        dst, src = wload(wg[i % 3][:, :, :], w_in.ap(), 0, D, (o_ga if sg == 0 else o_gb) + s0_, GSW)
                        S.dma('pool', dst, src, writes=[wgB[i % 3]])
                        gd[0] += 1

                gcnt = [0]
                for gi_, (st_, sg, s0_) in enumerate(gslabs):
                    wgi, wgBi = wg[gi_ % 3], wgB[gi_ % 3]
                    stg_, stgB_ = gst[gi_ % 2], gstB[gi_ % 2]
                    for ci in range(GSW // 128):
                        for hf in range(GNH):
                            k_ = gcnt[0]
                            gcnt[0] += 1
                            gbk = 4 + (k_ % 2)
                            ex_, exB_ = gex[k_ % 2], gexB[k_ % 2]

                            def first(gi_=gi_, st_=st_, s0_=s0_, sg=sg, ci=ci, hf=hf, gbk=gbk, wgBi=wgBi):
                                gate_dma(gi_ + 2)
                                if ci == 0 and hf == 0 and gi_ % 4 != 3:
                                    conv_run(1)
                                if sg == 0 and s0_ == 0 and ci == 0 and hf == 0:
                                    for k4 in range(0, KC, 8):
                                        k4e = min(KC, k4 + 8)
                                        S.dma('sp', hTg[:, k4:k4e, :],
                                              hT_d.ap()[k4 * 128:k4e * 128, T + st_ * GTS:T + (st_ + 1) * GTS].rearrange(
                                                  "(k p) t -> p k t", p=128), writes=[hTgB])
                                S.begin('pe', reads=[wgBi, hTgB], writes=[PSB[gbk]])
                            gsteps.append(first)
                            for kc in range(KC):
                                fn = (lambda e, kc=kc, ci=ci, hf=hf, gbk=gbk, wgi=wgi: e.matmul(
                                    PS[gbk][:, 0:GPW], lhsT=wgi[:, kc, ci * 128:(ci + 1) * 128],
                                    rhs=hTg[:, kc, hf * GPW:(hf + 1) * GPW], start=(kc == 0), stop=(kc == KC - 1)))
                                if kc < KC - 1:
                                    gsteps.append(lambda fn=fn: S.raw('pe', fn))
                                    continue

                                def last(fn=fn, ci=ci, hf=hf, gbk=gbk, wgBi=wgBi, ex_=ex_, exB_=exB_, stg_=stg_, stgB_=stgB_,
                                         st_=st_, sg=sg, s0_=s0_):
                                    S.group('pe', [fn], reads=[wgBi, hTgB], writes=[PSB[gbk]])
                                    S.op('act', lambda e: e.activation(out=ex_[:, :], in_=PS[gbk][:, 0:GPW], func=AF.Exp,
                                                                       scale=-1.0), reads=[PSB[gbk]], writes=[exB_])
                                    S.op('dve', lambda e: e.tensor_scalar(out=ex_[:, :], in0=ex_[:, :], scalar1=1.0,
                                                                          scalar2=None, op0=ALU.add),
                                         reads=[exB_], writes=[exB_])
                                    S.op('dve', lambda e: e.reciprocal(out=stg_[:, ci, hf * GPW:(hf + 1) * GPW], in_=ex_[:, :]),
                                         reads=[exB_], writes=[stgB_])
                                    if ci == GSW // 128 - 1 and hf == GNH - 1:
                                        dd = gT_d.ap()[sg * D + s0_:sg * D + s0_ + GSW, st_ * GTS:(st_ + 1) * GTS]
                                        S.dma('sp', dd.rearrange("(k p) t -> p k t", p=128), stg_[:, :, :], reads=[stgB_])
                                gsteps.append(last)
            gpos = [0]

            def gate_run(n):
                for f_ in gsteps[gpos[0]:gpos[0] + n]:
                    f_()
                gpos[0] = min(len(gsteps), gpos[0] + n)

            def load_head(h):
                S.dma('sp', kTh[h % 2][:, :], kT_d.ap()[h * 128:(h + 1) * 128, :], writes=[kTB[h % 2]])
                S.dma('sp', qTh[h % 2][:, :], qT_d.ap()[h * 128:(h + 1) * 128, :], writes=[qB[h % 2]])
                S.dma('sp', vh[h % 2][:, :, :], v_d.ap()[:, h * 128:(h + 1) * 128].rearrange("(t p) d -> p t d", p=128),
                      writes=[vB[h % 2]])

            blocks = []
            nq = 0
            for h in range(c.NH_SB):
                for j in range(NQT):
                    kbs = list(range(c.NTO + (j + 1) * NDB - 1, -1, -1))
                    for bi, kb in enumerate(kbs):
                        blocks.append(dict(h=h, j=j, kb=kb, bi=bi, n=len(kbs), nq=nq, idx=len(blocks)))
                    nq += 1
            NBK = len(blocks)
            st = {}

            def s123(b):
                h, j, kb, i = b['h'], b['j'], b['kb'], b['idx']
                kT, kB_ = kTh[h % 2], kTB[h % 2]
                qT, qB_ = qTh[h % 2], qB[h % 2]
                qs_ = qT[:, j * TQ:(j + 1) * TQ]
                kblk = kT[:, kb * 128:(kb + 1) * 128]
                ctx = kb < c.NTO
                dg = kb - (c.NTO + j * NDB)
                bias = cbias[:, 0:1] if ctx else c_zero[:, 0:1]
                zb = i % 2
                ee, eeB = e_[i % NE], eB[i % NE]
                l_, lB_ = lp[i % NL], lpB[i % NL]
                S.op('pe', lambda e: e.matmul(PS[zb][:, 0:TQ], lhsT=kblk, rhs=qs_, start=True, stop=True),
                     reads=[kB_, qB_], writes=[PSB[zb]])
                S.op('act', lambda e: e.activation(out=ee[:, :], in_=PS[zb][:, 0:TQ], func=AF.Exp, bias=bias, scale=1.0),
                     reads=[PSB[zb], CONST], writes=[eeB])
                S.op('act', lambda e: e.activation(out=l_[:, :], in_=ee[:, :], func=AF.Ln, bias=c_one[:, 0:1], scale=1.0),
                     reads=[eeB, CONST], writes=[lB_])
                if dg >= 0:
                    S.op('dve', lambda e: e.tensor_tensor(out=l_[:, :], in0=l_[:, :], in1=sbm[:, dg, :], op=ALU.mult),
                         reads=[lB_, CONST], writes=[lB_])
                b.update(qs_=qs_, kblk=kblk, bias=bias, dg=dg, l_=l_, lB_=lB_, kB_=kB_, qB_=qB_)

            def s45(b):
                i = b['idx']
                bb = 2 + (i % 2)
                a_, aB_ = am[i % NA], amB[i % NA]
                l_, lB_, kblk, qs_ = b['l_'], b['lB_'], b['kblk'], b['qs_']
                la = None if b['bi'] == 0 else st['la']
                fns = [lambda e: e.matmul(PS[bb][:, 0:TQ], lhsT=kblk, rhs=qs_, start=True, stop=False),
                       lambda e: e.matmul(PS[bb][:, 0:TQ], lhsT=negtri[:, :], rhs=l_[:, :], start=False, stop=(la is None))]
                rds = [b['kB_'], b['qB_'], lB_, CONST]
                if la is not None:
                    la_t, la_B = la
                    fns.append(lambda e: e.matmul(PS[bb][:, 0:TQ], lhsT=negones[:, :], rhs=la_t[:, :], start=False, stop=True))
                    rds.append(la_B)
                S.group('pe', fns, reads=rds, writes=[PSB[bb]])
                bias, dg = b['bias'], b['dg']
                S.op('act', lambda e: e.activation(out=a_[:, :], in_=PS[bb][:, 0:TQ], func=AF.Exp, bias=bias, scale=1.0),
                     reads=[PSB[bb], CONST], writes=[aB_])
                if dg >= 0:
                    S.op('dve', lambda e: e.tensor_tensor(out=a_[:, :], in0=a_[:, :], in1=sbm[:, dg, :], op=ALU.mult),
                         reads=[aB_, CONST], writes=[aB_])
                b.update(a_=a_, aB_=aB_)
                if b['bi'] < b['n'] - 1:
                    if la is None:
                        st['la'] = (l_, lB_)
                    else:
                        la_t, la_B = la
                        k3 = st.get('k3', 0)
                        st['k3'] = k3 + 1
                        ln_, lnB = lacc[k3 % 3], laccB[k3 % 3]
                        S.op('pool', lambda e: e.tensor_tensor(out=ln_[:, :], in0=la_t[:, :], in1=l_[:, :], op=ALU.add),
                             reads=[la_B, lB_], writes=[lnB])
                        st['la'] = (ln_, lnB)

            def s6(b):
                h, j, kb = b['h'], b['j'], b['kb']
                ybk = 6 + (b['nq'] % 2)
                vv, vB_ = vh[h % 2], vB[h % 2]
                a_, aB_ = b['a_'], b['aB_']
                first, last = b['bi'] == 0, b['bi'] == b['n'] - 1
                S.op('pe', lambda e: e.matmul(PS[ybk][:, 0:TQ], lhsT=vv[:, kb, :], rhs=a_[:, :], start=first, stop=last),
                     reads=[vB_, aB_], writes=[PSB[ybk]])
                if last:
                    ys, ysB = yst[b['nq'] % 2], ystB[b['nq'] % 2]
                    S.op('dve', lambda e: e.tensor_copy(out=ys[:, :], in_=PS[ybk][:, 0:TQ]), reads=[PSB[ybk]], writes=[ysB])
                    S.dma('sp', yaT_d.ap()[h * 128:(h + 1) * 128, j * TQ:(j + 1) * TQ], ys[:, :], reads=[ysB])

            first3 = {}
            for b in blocks:
                first3.setdefault(b['h'], b['idx'])
            load_head(0)
            gper = -(-len(gsteps) // max(1, NBK))
            for t in range(NBK + 2):
                if t < NBK:
                    s123(blocks[t])
                if 0 <= t - 1 < NBK:
                    s45(blocks[t - 1])
                if 0 <= t - 2 < NBK:
                    s6(blocks[t - 2])
                for h_, f_ in first3.items():
                    if t == f_ + 2 and h_ + 1 < c.NH_SB:
                        load_head(h_ + 1)
                gate_run(gper)
            gate_run(len(gsteps))
            S.barrier(PSB + [CONST])

        with contextlib.ExitStack() as ph:
            dm = sb("dm", [128, NTAP, 128], BF16, ph)
            dmc = sb("dmc", [128, NTAP, 128], BF16, ph)
            S.dma('sp', dm[:, :, :], k_dmask.ap(), writes=[CONST])
            S.dma('sp', dmc[:, :, :], k_dmaskc.ap(), writes=[CONST])
            kd = [[sb(f"kd{i}_{g}", [128, L], BF16, ph) for g in range(c.NG)] for i in range(2)]
            qd = [[sb(f"qd{i}_{g}", [128, T], BF16, ph) for g in range(c.NG)] for i in range(2)]
            vdt = [[sb(f"vd{i}_{g}", [128, c.NT, 132], BF16, ph) for g in range(c.NG)] for i in range(2)]
            hdB = [Buf(f"hd{i}") for i in range(2)]
            NPE = 4
            pe_ = [sb(f"pe{i}", [128, 512], F32, ph) for i in range(NPE)]
            peB = [Buf(f"pe{i}") for i in range(NPE)]
            pm = [sb(f"pm{i}", [128, 512], BF16, ph) for i in range(NPE)]
            pmB = [[Buf(f"pm{i}_{q}") for q in range(4)] for i in range(NPE)]
            rden = [sb(f"rden{i}", [128, 2], F32, ph) for i in range(2)]
            rdB = [Buf(f"rden{i}") for i in range(2)]
            yq = [sb(f"yq{i}", [128, 128], BF16, ph) for i in range(2)]
            yqB = [Buf(f"yq{i}") for i in range(2)]
            ybst = [sb(f"ybst{i}", [128, T], BF16, ph) for i in range(2)]
            ybB = [Buf(f"ybst{i}") for i in range(2)]
            wsl4 = [sb(f"w4_{i}", [128, KC, SW0], BF16, ph) for i in range(3)]
            wsl4B = [Buf(f"w4_{i}") for i in range(3)]
            rest4 = list(range(nfirst, nsl0))

            def load_dhead(hd):
                i2 = hd % 2
                for g in range(c.NG):
                    row = g * c.D_DIL + hd * 128
                    lo_t = max(0, T - c.groups[g][0])
                    lo_b = lo_t // 128
                    S.dma('sp', kd[i2][g][:, lo_t:], kdT_d.ap()[row:row + 128, lo_t:], writes=[hdB[i2]])
                    S.dma('sp', qd[i2][g][:, :], qdT_d.ap()[row:row + 128, :], writes=[hdB[i2]])
                    S.dma('sp', vdt[i2][g][:, lo_b:, 0:128],
                          vd_d.ap()[lo_t:, row:row + 128].rearrange("(t p) d -> p t d", p=128), writes=[hdB[i2]])
                    S.op('pool', lambda e, i2=i2, g=g: e.memset(vdt[i2][g][:, :, 128:129], 1.0), writes=[hdB[i2]])

            tgs = []
            nqb = 0
            for hd in range(c.NH_DIL):
                for qi in range(c.NTO):
                    for t4 in range(0, NTAP, 4):
                        tgs.append(dict(hd=hd, qi=qi, t4=t4, grp=c.taps[t4:t4 + 4], nqb=nqb, idx=len(tgs)))
                    nqb += 1
            NTG = len(tgs)

            def sA(b):
                hd, qi, t4, grp, i = b['hd'], b['qi'], b['t4'], b['grp'], b['idx']
                i2 = hd % 2
                sbk = i % 3
                p_e, p_eB = pe_[i % NPE], peB[i % NPE]
                p_m, p_mB = pm[i % NPE], pmB[i % NPE]
                fns = []
                for ti, (g, dl) in enumerate(grp):
                    kb = c.NTO + qi - dl
                    fns.append(lambda e, ti=ti, g=g, kb=kb: e.matmul(
                        PS[sbk][:, ti * 128:(ti + 1) * 128], lhsT=kd[i2][g][:, kb * 128:(kb + 1) * 128],
                        rhs=qd[i2][g][:, qi * 128:(qi + 1) * 128], start=True, stop=True))
                S.group('pe', fns, reads=[hdB[i2]], writes=[PSB[sbk]])
                w = len(grp) * 128
                S.op('act', lambda e: e.activation(out=p_e[:, 0:w], in_=PS[sbk][:, 0:w], func=AF.Exp),
                     reads=[PSB[sbk]], writes=[p_eB])
                for ti, (g, dl) in enumerate(grp):
                    kb = c.NTO + qi - dl
                    mk = dmc if kb < c.NTO else dm
                    en = 'dve'
                    S.op(en, lambda e, ti=ti, mk=mk: e.tensor_tensor(
                        out=p_m[:, ti * 128:(ti + 1) * 128], in0=p_e[:, ti * 128:(ti + 1) * 128],
                        in1=mk[:, t4 + ti, :], op=ALU.mult), reads=[p_eB, CONST], writes=[p_mB[ti]])
                b.update(p_m=p_m, p_mB=p_mB)

            def sB(b):
                hd, qi, t4, grp, nq_ = b['hd'], b['qi'], b['t4'], b['grp'], b['nqb']
                i2 = hd % 2
                obk = 6 + (nq_ % 2)
                p_m, p_mB = b['p_m'], b['p_mB']
                fns = []
                for ti, (g, dl) in enumerate(grp):
                    kb = c.NTO + qi - dl
                    tix = t4 + ti
                    fns.append(lambda e, ti=ti, g=g, kb=kb, tix=tix: e.matmul(
                        PS[obk][:, 0:129], lhsT=p_m[:, ti * 128:(ti + 1) * 128], rhs=vdt[i2][g][:, kb, 0:129],
                        start=(tix == 0), stop=(tix == NTAP - 1)))
                S.group('pe', fns, reads=p_mB + [hdB[i2]], writes=[PSB[obk]])
                if t4 + 4 >= NTAP:
                    y_, yB_ = yq[nq_ % 2], yqB[nq_ % 2]
                    rd, rdB_ = rden[nq_ % 2], rdB[nq_ % 2]
                    S.op('dve', lambda e: e.reciprocal(out=rd[:, 0:1], in_=PS[obk][:, 128:129]),
                         reads=[PSB[obk]], writes=[rdB_])
                    S.op('dve', lambda e: e.tensor_scalar(out=y_[:, :], in0=PS[obk][:, 0:128], scalar1=rd[:, 0:1],
                                                          scalar2=None, op0=ALU.mult),
                         reads=[PSB[obk], rdB_], writes=[yB_])
                    tbk = 4 + (nq_ % 2)
                    psb = PS[tbk].bitcast(BF16)
                    S.op('pe', lambda e: e.transpose(psb[:, 0:128], y_[:, :], identb[:, :]),
                         reads=[yB_, CONST], writes=[PSB[tbk]])
                    S.op('act', lambda e: e.activation(out=ybst[i2][:, qi * 128:(qi + 1) * 128], in_=psb[:, 0:128],
                                                       func=AF.Copy), reads=[PSB[tbk]], writes=[ybB[i2]])
                    if qi == c.NTO - 1:
                        S.dma('sp', ybT_d.ap()[hd * 128:(hd + 1) * 128, :], ybst[i2][:, :], reads=[ybB[i2]])

            first4 = {}
            for b in tgs:
                first4.setdefault(b['hd'], b['idx'])
            load_dhead(0)
            for t in range(NTG + 2):
                if t < NTG:
                    sA(tgs[t])
                if 0 <= t - 2 < NTG:
                    sB(tgs[t - 2])
                for h_, f_ in first4.items():
                    if t == f_ + 2 and h_ + 1 < c.NH_DIL:
                        load_dhead(h_ + 1)
                if t < NTG:
                    lo_ = (t * len(rest4)) // NTG
                    hi_ = ((t + 1) * len(rest4)) // NTG
                    for sl in rest4[lo_:hi_]:
                        mod_slab(sl, wsl4, wsl4B, 3, nsl0 - 1)
            mod_finish(2 * KC, 6 * KC, MB["mod2"], A2, nw2, SC2, 3)
            if "mod_d" in debug_outs:
                S.dma('sp', mod_d.ap(), modsb[:, :], reads=[MB["mod1"], MB["mod2"]])
            S.barrier(PSB + [CONST])

        KA = c.D_SB // 128
        KB = c.D_DIL // 128
        TS = min(1024, T)
        PW = min(512, TS)
        NHF = TS // PW
        with contextlib.ExitStack() as ph:
            yaT = sb("yaT", [128, KA, TS], BF16, ph)
            ybT = sb("ybT", [128, KB, TS], BF16, ph)
            yB5 = Buf("y5")
            mT = sb("mT", [128, KC, TS], BF16, ph)
            mBs = [Buf(f"mT{i}") for i in range(KC)]
            SW5 = 256
            wa = [sb(f"wa{i}", [128, KA, SW5], BF16, ph) for i in range(2)]
            wbb = [sb(f"wb{i}", [128, KB, SW5], BF16, ph) for i in range(2)]
            w5B = [Buf(f"w5{i}") for i in range(2)]
            wo = [sb(f"wo{i}", [128, KC, SW5], BF16, ph) for i in range(2)]
            woB = [Buf(f"wo{i}") for i in range(2)]
            gt = [sb(f"gt{i}", [128, 2, PW], F32, ph) for i in range(2)]
            gtB = [Buf(f"gt{i}") for i in range(2)]
            t5 = [sb(f"t5{i}", [128, PW], F32, ph) for i in range(2)]
            t5B = [Buf(f"t5{i}") for i in range(2)]
            xr = [sb(f"xr{i}", [128, PW], F32, ph) for i in range(2)]
            xrB = [Buf(f"xr{i}") for i in range(2)]
            x1s = [sb(f"x1s{i}", [128, PW], F32, ph) for i in range(2)]
            x1B = [Buf(f"x1s{i}") for i in range(2)]
            nw5 = 0
            ng = 0
            nbk = 0
            sw5 = min(SW5, D)
            wl5 = []
            for st in range(T // TS):
                wl5 += [('ab', s0) for s0 in range(0, D, sw5)] + [('o', s0) for s0 in range(0, D, sw5)]
            cnt5 = dict(ab=0, o=0, g=0, dab=0, do=0, d=0)

            def ensure5(upto):
                while cnt5['d'] <= min(upto, len(wl5) - 1):
                    kind5, s0_ = wl5[cnt5['d']]
                    if kind5 == 'ab':
                        i5_ = cnt5['dab'] % 2
                        cnt5['dab'] += 1
                        dst, src = wload(wa[i5_][:, :, 0:sw5], w_o_sb.ap(), 0, c.D_SB, s0_, sw5)
                        S.dma('pool', dst, src, writes=[w5B[i5_]])
                        dst, src = wload(wbb[i5_][:, :, 0:sw5], w_o_dil.ap(), 0, c.D_DIL, s0_, sw5)
                        S.dma('pool', dst, src, writes=[w5B[i5_]])
                    else:
                        io_ = cnt5['do'] % 2
                        cnt5['do'] += 1
                        dst, src = wload(wo[io_][:, :, 0:sw5], w_out.ap(), 0, D, s0_, sw5)
                        S.dma('pool', dst, src, writes=[woB[io_]])
                    cnt5['d'] += 1
                    conv_run(1)

            ensure5(0)

            def load_y(st_):
                t_ = st_ * TS
                S.dma('sp', yaT[:, :, :], yaT_d.ap()[:, t_:t_ + TS].rearrange("(k p) t -> p k t", p=128), writes=[yB5])
                S.dma('sp', ybT[:, :, :], ybT_d.ap()[:, t_:t_ + TS].rearrange("(k p) t -> p k t", p=128), writes=[yB5])

            load_y(0)
            for st in range(T // TS):
                to0 = st * TS
                sw = min(SW5, D)
                for s0 in range(0, D, sw):
                    i5 = cnt5['ab'] % 2
                    cnt5['ab'] += 1
                    ensure5(cnt5['g'] + 1)
                    cnt5['g'] += 1
                    for ci in range(sw // 128):
                        fch = (s0 // 128) + ci
                        for hf in range(NHF):
                            ig = ng % 2
                            ng += 1
                            tk0 = to0 + hf * PW
                            S.dma('sp', gt[ig][:, 0, :], gT_d.ap()[fch * 128:(fch + 1) * 128, tk0:tk0 + PW], writes=[gtB[ig]])
                            S.dma('sp', gt[ig][:, 1, :], gT_d.ap()[D + fch * 128:D + (fch + 1) * 128, tk0:tk0 + PW],
                                  writes=[gtB[ig]])
                            ba = nbk % 6
                            bbk = (nbk + 1) % 6
                            nbk += 2
                            fns = [lambda e, kc=kc, ci=ci, hf=hf, i5=i5, ba=ba: e.matmul(
                                PS[ba][:, 0:PW], lhsT=wa[i5][:, kc, ci * 128:(ci + 1) * 128],
                                rhs=yaT[:, kc, hf * PW:(hf + 1) * PW], start=(kc == 0), stop=(kc == KA - 1)) for kc in range(KA)]
                            S.group('pe', fns, reads=[w5B[i5], yB5], writes=[PSB[ba]])
                            fns = [lambda e, kc=kc, ci=ci, hf=hf, i5=i5, bbk=bbk: e.matmul(
                                PS[bbk][:, 0:PW], lhsT=wbb[i5][:, kc, ci * 128:(ci + 1) * 128],
                                rhs=ybT[:, kc, hf * PW:(hf + 1) * PW], start=(kc == 0), stop=(kc == KB - 1)) for kc in range(KB)]
                            S.group('pe', fns, reads=[w5B[i5], yB5], writes=[PSB[bbk]])
                            ta, taB = t5[0], t5B[0]
                            tb_, tbB = t5[1], t5B[1]
                            S.op('dve', lambda e, ta=ta, ba=ba, ig=ig: e.tensor_tensor(out=ta[:, :], in0=PS[ba][:, 0:PW],
                                                                                       in1=gt[ig][:, 0, :], op=ALU.mult),
                                 reads=[PSB[ba], gtB[ig]], writes=[taB])
                            S.op('dve', lambda e, tb_=tb_, bbk=bbk, ig=ig: e.tensor_tensor(out=tb_[:, :], in0=PS[bbk][:, 0:PW],
                                                                                           in1=gt[ig][:, 1, :], op=ALU.mult),
                                 reads=[PSB[bbk], gtB[ig]], writes=[tbB])
                            S.op('dve', lambda e, ta=ta, tb_=tb_, fch=fch, hf=hf: e.tensor_tensor(
                                out=mT[:, fch, hf * PW:(hf + 1) * PW], in0=ta[:, :], in1=tb_[:, :], op=ALU.add),
                                reads=[taB, tbB], writes=[mBs[fch]])
                if st + 1 < T // TS:
                    load_y(st + 1)
                for s0 in range(0, D, sw):
                    io = cnt5['o'] % 2
                    cnt5['o'] += 1
                    ensure5(cnt5['g'] + 1)
                    cnt5['g'] += 1
                    for ci in range(sw // 128):
                        fch = (s0 // 128) + ci
                        for hf in range(NHF):
                            ix = ng % 2
                            ng += 1
                            tk0 = to0 + hf * PW
                            S.dma('sp', xr[ix][:, :], xT_d.ap()[fch * 128:(fch + 1) * 128, tk0:tk0 + PW], writes=[xrB[ix]])
                            bk = nbk % 6
                            nbk += 1
                            fns = [lambda e, kc=kc, ci=ci, hf=hf, io=io, bk=bk: e.matmul(
                                PS[bk][:, 0:PW], lhsT=wo[io][:, kc, ci * 128:(ci + 1) * 128],
                                rhs=mT[:, kc, hf * PW:(hf + 1) * PW], start=(kc == 0), stop=(kc == KC - 1)) for kc in range(KC)]
                            S.group('pe', fns, reads=[woB[io]] + mBs, writes=[PSB[bk]])
                            S.op('dve', lambda e, ix=ix, bk=bk, fch=fch: e.scalar_tensor_tensor(
                                out=x1s[ix][:, :], in0=PS[bk][:, 0:PW], scalar=modsb[:, G1 + fch:G1 + fch + 1],
                                in1=xr[ix][:, :], op0=ALU.mult, op1=ALU.add), reads=[PSB[bk], xrB[ix]], writes=[x1B[ix]])
                            S.dma('sp', x1T_d.ap()[fch * 128:(fch + 1) * 128, tk0:tk0 + PW], x1s[ix][:, :], reads=[x1B[ix]])
            conv_run(len(conv_steps))
            S.barrier(PSB + [CONST])

        TF = c.TF
        NFB = TF // 128
        with contextlib.ExitStack() as ph:
            acc = sb("acc", [128, KC, TF], F32, ph)
            accB = [Buf(f"acc{i}") for i in range(KC)]
            h2 = sb("h2", [128, KC, TF], BF16, ph)
            h2B = Buf("h2")
            NFC = SWF // 128
            w1 = [sb(f"w1_{i}", [128, KC, SWF], BF16, ph) for i in range(2)]
            w1B = [Buf(f"w1_{i}") for i in range(2)]
            w2 = [sb(f"w2f_{i}", [128, NFC, D], BF16, ph) for i in range(2)]
            w2B = [Buf(f"w2f_{i}") for i in range(2)]
            ut = [sb(f"ut{i}", [128, NFC, TF], BF16, ph) for i in range(2)]
            utB = [Buf(f"ut{i}") for i in range(2)]
            sqs = [sb(f"fsq{i}", [128, TF], F32, ph) for i in range(2)]
            sqB = [Buf(f"fsq{i}") for i in range(2)]
            tmps = [sb(f"ftm{i}", [128, TF], F32, ph) for i in range(2)]
            tmpB = [Buf(f"ftm{i}") for i in range(2)]
            rs = sb("frs", [128, TF], F32, ph)
            rstd = sb("frstd", [128, TF], F32, ph)
            rB = Buf("frs")
            ot = [sb(f"ot{i}", [128, D], F32, ph) for i in range(1)]
            otB = [Buf(f"ot{i}") for i in range(1)]
            nev = 0
            gl = [(ft, fg) for ft in range(T // TF) for fg in range(NGF)]
            wslot = {}

            def load_w(gi):
                iw = gi % 2
                ft, fg = gl[gi]
                S.dma('pool', w1[iw][:, :, :], w1b_d.ap()[fg], writes=[w1B[iw]])
                S.dma('pool', w2[iw][:, :, :],
                      w2b_d.ap()[fg * SWF:(fg + 1) * SWF, :].rearrange("(k p) f -> p k f", p=128), writes=[w2B[iw]])

            def ffn1_pieces(gi):
                iw = gi % 2
                u_, uB_ = ut[iw], utB[iw]
                steps = []
                for ci in range(NFC):
                    bk = (gi * NFC + ci) % 3

                    def first(ci=ci, bk=bk):
                        S.begin('pe', reads=[w1B[iw], h2B], writes=[PSB[bk]])
                    steps.append(first)
                    for kc in range(KC):
                        fn = (lambda e, kc=kc, ci=ci, bk=bk: e.matmul(
                            PS[bk][:, 0:TF], lhsT=w1[iw][:, kc, ci * 128:(ci + 1) * 128], rhs=h2[:, kc, :],
                            start=(kc == 0), stop=(kc == KC - 1)))
                        if kc < KC - 1:
                            steps.append(lambda fn=fn: S.raw('pe', fn))
                        else:
                            def last(fn=fn, ci=ci, bk=bk):
                                S.group('pe', [fn], reads=[w1B[iw], h2B], writes=[PSB[bk]])
                                sq, sB = sqs[ci % 2], sqB[ci % 2]
                                S.op('act', lambda e: e.activation(out=sq[:, :], in_=PS[bk][:, 0:TF], func=AF.Square),
                                     reads=[PSB[bk]], writes=[sB])
                                S.op('dve', lambda e: e.scalar_tensor_tensor(
                                    out=u_[:, ci, :], in0=PS[bk][:, 0:TF], scalar=0.0, in1=sq[:, :], op0=ALU.is_gt,
                                    op1=ALU.mult), reads=[PSB[bk], sB], writes=[uB_])
                            steps.append(last)
                return steps

            load_w(0)
            for gi, (ft, fg) in enumerate(gl):
                to0 = ft * TF
                iw = gi % 2
                u_, uB_ = ut[iw], utB[iw]
                if fg == 0:
                    for k4 in range(0, KC, 8):
                        k4e = min(KC, k4 + 8)
                        S.dma('sp', acc[:, k4:k4e, :],
                              x1T_d.ap()[k4 * 128:k4e * 128, to0:to0 + TF].rearrange("(k p) t -> p k t", p=128),
                              writes=accB[k4:k4e])
                    norm_mod(ph, "n2", acc, accB, TF, A2, SH2, h2, h2B, 7, sqs, sqB, tmps, tmpB, rs, rstd, rB)
                    for stp in ffn1_pieces(gi):
                        stp()
                if gi + 1 < len(gl):
                    load_w(gi + 1)
                nxt = ffn1_pieces(gi + 1) if (gi + 1 < len(gl) and gl[gi + 1][0] == ft) else []
                per = -(-len(nxt) // KC) if nxt else 0
                for fo in range(KC):
                    bk = 3 + ((gi * KC + fo) % 4)
                    fns = [lambda e, kc=kc, fo=fo, bk=bk, iw=iw, u_=u_: e.matmul(
                        PS[bk][:, 0:TF], lhsT=w2[iw][:, kc, fo * 128:(fo + 1) * 128], rhs=u_[:, kc, :],
                        start=(kc == 0), stop=(kc == NFC - 1)) for kc in range(NFC)]
                    S.group('pe', fns, reads=[w2B[iw], uB_], writes=[PSB[bk]])
                    S.op('dve', lambda e, fo=fo, bk=bk: e.scalar_tensor_tensor(
                        out=acc[:, fo, :], in0=PS[bk][:, 0:TF], scalar=modsb[:, G2 + fo:G2 + fo + 1],
                        in1=acc[:, fo, :], op0=ALU.mult, op1=ALU.add), reads=[PSB[bk], accB[fo]], writes=[accB[fo]])
                    for stp in nxt[fo * per:(fo + 1) * per]:
                        stp()
                if fg < NGF - 1:
                    continue
                norm_stats(acc, accB, TF, 7, sqs, sqB, rs, rstd, rB)
                for kc in range(KC):
                    S.op('dve', lambda e, kc=kc: e.scalar_tensor_tensor(
                        out=acc[:, kc, :], in0=acc[:, kc, :], scalar=nfwt[:, kc:kc + 1], in1=rstd[:, 0:TF],
                        op0=ALU.mult, op1=ALU.mult), reads=[accB[kc], rB, CONST], writes=[accB[kc]])
                cpb = min(4, KC)
                for tb in range(NFB):
                    o_, oB_ = ot[0], otB[0]
                    for g4 in range(KC // cpb):
                        bk = 3 + (nev % 4)
                        fns = [lambda e, i=i, g4=g4, bk=bk, tb=tb: e.transpose(
                            PS[bk][:, i * 128:(i + 1) * 128], acc[:, g4 * cpb + i, tb * 128:(tb + 1) * 128], identf[:, :])
                            for i in range(cpb)]
                        S.group('pe', fns, reads=accB[g4 * cpb:(g4 + 1) * cpb] + [CONST], writes=[PSB[bk]])
                        dsto = o_[:, g4 * cpb * 128:(g4 + 1) * cpb * 128]
                        if nev % 2 == 0:
                            S.op('act', lambda e, dsto=dsto, bk=bk: e.activation(out=dsto, in_=PS[bk][:, 0:cpb * 128], func=AF.Copy),
                                 reads=[PSB[bk]], writes=[oB_])
                        else:
                            S.op('dve', lambda e, dsto=dsto, bk=bk: e.tensor_copy(out=dsto, in_=PS[bk][:, 0:cpb * 128]),
                                 reads=[PSB[bk]], writes=[oB_])
                        nev += 1
                    S.dma('sp', out_d.ap()[to0 + tb * 128:to0 + (tb + 1) * 128, :], o_[:, :], reads=[oB_])
            S.barrier(PSB + [CONST])

        with nc.Block() as block:
            @block.tensor
            def _(eng):
                for f in engs['pe'].ops:
                    f(eng)

            @block.scalar
            def _(eng):
                for f in engs['act'].ops:
                    f(eng)

            @block.vector
            def _(eng):
                for f in engs['dve'].ops:
                    f(eng)

            @block.gpsimd
            def _(eng):
                for f in engs['pool'].ops:
                    f(eng)

            @block.sync
            def _(eng):
                for f in engs['sp'].ops:
                    f(eng)
    return nc


def make_consts(cfg, s):
    c = cfg
    bf = ml_dtypes.bfloat16
    j = np.arange(128)
    k = {}
    k["k_identf"] = np.eye(128, dtype=np.float32)
    k["k_identb"] = np.eye(128, dtype=np.float32).astype(bf)
    k["k_negtri"] = (-(j[:, None] >= j[None, :]).astype(np.float32)).astype(bf)
    k["k_negones"] = (-np.ones((128, 128), np.float32)).astype(bf)
    k["k_onesf"] = np.ones((128, 128), np.float32)
    ndb = c.TQ // 128
    q = np.arange(c.TQ)
    sbm = np.zeros((128, ndb, c.TQ), np.float32)
    for dg in range(ndb):
        sbm[:, dg, :] = (dg * 128 + j[:, None]) < q[None, :]
    k["k_sbmask"] = sbm.astype(bf)
    ntap = len(c.taps)
    dm = np.zeros((128, ntap, 128), np.float32)
    for ti, (g, dl) in enumerate(c.taps):
        w, r = c.groups[g]
        diff = dl * 128 + (j[None, :] - j[:, None])
        dm[:, ti, :] = (diff >= 0) & (diff <= w) & (diff % r == 0)
    k["k_dmask"] = dm.astype(bf)
    k["k_dmaskc"] = (dm * float(s)).astype(bf)
    k["k_cbias"] = np.full((128, 1), 0.0 if s == 1 else -60.0, np.float32)
    half = 64
    invf = (10000.0 ** (-np.arange(half, dtype=np.float32) / half)).astype(np.float32)
    k["k_invf"] = np.broadcast_to(invf[None, :], (128, half)).copy()
    return k


def core_inputs(cfg, b, s, x, cvec, positions, ada_w, ada_b, norm_mix_w, w_in, w_o_sb, w_o_dil, w_out,
                norm_mlp_w, w_ff1, w_ff2, norm_final_w):
    c = cfg
    T, KC = c.T, c.KC
    own = slice(s * T, (s + 1) * T)
    oth = slice((1 - s) * T, (2 - s) * T)
    m = {}
    m["xl"] = np.ascontiguousarray(np.concatenate([x[b, oth], x[b, own]], axis=0))
    m["c_t"] = np.ascontiguousarray(cvec[b].reshape(KC, 128).T)
    pl = np.concatenate([positions[b, oth], positions[b, own]], axis=0).astype(np.int32)
    m["pos_t"] = np.ascontiguousarray(pl.reshape(c.NT, 128).T)
    m["ada_w"] = ada_w
    m["adab_t"] = np.ascontiguousarray(ada_b.reshape(6 * KC, 128).T)
    m["nw1_t"] = np.ascontiguousarray(norm_mix_w.reshape(KC, 128).T)
    m["nw2_t"] = np.ascontiguousarray(norm_mlp_w.reshape(KC, 128).T)
    m["nfw_t"] = np.ascontiguousarray(norm_final_w.reshape(KC, 128).T)
    m["w_in"] = w_in
    m["w_o_sb"] = w_o_sb
    m["w_o_dil"] = w_o_dil
    m["w_out"] = w_out
    m["w_ff1"] = w_ff1
    m["w_ff2"] = w_ff2
    m.update(make_consts(c, s))
    return m


def kernel(x, c, positions, ada_w, ada_b, norm_mix_w, w_in, w_o_sb, w_o_dil, w_out, norm_mlp_w, w_ff1, w_ff2,
           norm_final_w):
    cfg = REAL
    f = lambda a: np.asarray(a)
    x, cvec, positions = f(x), f(c), f(positions)
    args = [f(a)[0] for a in (ada_w, ada_b, norm_mix_w, w_in, w_o_sb, w_o_dil, w_out, norm_mlp_w, w_ff1, w_ff2)]
    ada_w, ada_b, norm_mix_w, w_in, w_o_sb, w_o_dil, w_out, norm_mlp_w, w_ff1, w_ff2 = args
    nfw = f(norm_final_w)
    nc = build(cfg)
    in_maps = []
    for core in range(8):
        b, s = core // 2, core % 2
        in_maps.append(core_inputs(cfg, b, s, x, cvec, positions, ada_w, ada_b, norm_mix_w, w_in, w_o_sb, w_o_dil,
                                   w_out, norm_mlp_w, w_ff1, w_ff2, nfw))
    res = run_bass_kernel_spmd(nc, in_maps, core_ids=list(range(8)))
    out = np.zeros((4, 2 * cfg.T, cfg.D), np.float32)
    for core in range(8):
        b, s = core // 2, core % 2
        out[b, s * cfg.T:(s + 1) * cfg.T] = res.results[core]["out"]
    return out
```
